# Optimizing a Trainium2 kernel written in Bass

```python
import math, functools
import jax, jax.numpy as jnp
from jax import lax
import numpy as np

D_MODEL = 1024
BATCH = 16
SEQ = 2048
DEPTH = 1

D_MIX = D_MODEL
HG_WIDTH = D_MIX // 2
CV_WIDTH = D_MIX - HG_WIDTH
HG_HEAD_K = 128
HG_HEADS = HG_WIDTH // HG_HEAD_K
HG_HEAD_V = HG_WIDTH // HG_HEADS
CHUNK = 64
CV_KERNEL = 31
CV_GROUPS = 8
FFN_KERNEL = 3
D_FF = 2752
N_MOD = 6
EPS = 1e-6
IN_COLS = 4 * HG_WIDTH + 2 * CV_WIDTH

kernel_name = "hymba_style_hgrn2_conformer_convglu_adaln"


def rms_norm(x, g):
    x32 = x.astype(jnp.float32)
    y = x32 * lax.rsqrt(jnp.mean(x32 * x32, axis=-1, keepdims=True) + EPS)
    return (y * g.astype(jnp.float32)).astype(x.dtype)


def causal_dwconv(u, w, b):
    kw = w.shape[0]
    y = lax.conv_general_dilated(
        u, w[:, None, :].astype(u.dtype), window_strides=(1,),
        padding=[(kw - 1, 0)], dimension_numbers=("NWC", "WIO", "NWC"),
        feature_group_count=u.shape[-1])
    return y + b.astype(u.dtype)


def hgrn2_recurrence(q, f_logit, i, lb):
    B, T, H, K = q.shape
    V = i.shape[-1]
    n = T // CHUNK
    q32 = q.astype(jnp.float32)
    lb32 = lb.astype(jnp.float32)
    f = lb32 + (1.0 - lb32) * jax.nn.sigmoid(f_logit.astype(jnp.float32))
    logf = jnp.log(f)
    k32 = 1.0 - f
    v32 = i.astype(jnp.float32)

    def to_chunks(a):
        return a.reshape(B, n, CHUNK, H, a.shape[-1]).transpose(1, 0, 3, 2, 4)

    mask = jnp.tril(jnp.ones((CHUNK, CHUNK), dtype=bool))[:, :, None]

    def step(S, inp):
        qc, kc, vc, gc = inp
        b = jnp.cumsum(gc, axis=2)
        o_inter = jnp.einsum('bhtk,bhkv->bhtv', qc * jnp.exp(b), S)
        diff = b[:, :, :, None, :] - b[:, :, None, :, :]
        decay = jnp.exp(jnp.where(mask, diff, -jnp.inf))
        A = jnp.einsum('bhtk,bhsk,bhtsk->bhts', qc, kc, decay)
        o = o_inter + jnp.einsum('bhts,bhsv->bhtv', A, vc)
        b_last = b[:, :, -1:, :]
        S_new = (jnp.exp(b_last[:, :, 0, :])[..., None] * S
                 + jnp.einsum('bhsk,bhsv->bhkv', kc * jnp.exp(b_last - b), vc))
        return S_new, o

    S0 = jnp.zeros((B, H, K, V), jnp.float32)
    _, o = lax.scan(step, S0, (to_chunks(q32), to_chunks(k32), to_chunks(v32), to_chunks(logf)))
    return o.transpose(1, 0, 3, 2, 4).reshape(B, T, H, V)


def setup_inputs(seed: int = 0) -> dict:
    key = jax.random.key(seed)
    ks = jax.random.split(key, 20)
    f32 = jnp.float32
    nrm = lambda k, s, sc: jax.random.normal(k, s, f32) * sc
    return {
        "x": nrm(ks[0], (BATCH, SEQ, D_MODEL), 1.0),
        "c": nrm(ks[1], (BATCH, D_MODEL), 1.0),
        "lb_table": 1.0 + nrm(ks[2], (DEPTH + 1, HG_WIDTH), 0.1),
        "w_ada": nrm(ks[3], (DEPTH, D_MODEL, N_MOD * D_MODEL), D_MODEL ** -0.5),
        "b_ada": nrm(ks[4], (DEPTH, N_MOD * D_MODEL), 0.01),
        "norm1_g": 1.0 + nrm(ks[5], (DEPTH, D_MODEL), 0.02),
        "w_in": nrm(ks[6], (DEPTH, D_MODEL, IN_COLS), D_MODEL ** -0.5),
        "hgrn_norm_g": 1.0 + nrm(ks[7], (DEPTH, HG_HEAD_V), 0.02),
        "conv_w": nrm(ks[8], (DEPTH, CV_KERNEL, CV_WIDTH), CV_KERNEL ** -0.5),
        "conv_b": nrm(ks[9], (DEPTH, CV_WIDTH), 0.01),
        "conv_norm_g": 1.0 + nrm(ks[10], (DEPTH, CV_WIDTH), 0.02),
        "conv_norm_b": nrm(ks[11], (DEPTH, CV_WIDTH), 0.01),
        "w_out": nrm(ks[12], (DEPTH, D_MIX, D_MODEL), D_MIX ** -0.5),
        "norm2_g": 1.0 + nrm(ks[13], (DEPTH, D_MODEL), 0.02),
        "w_gu": nrm(ks[14], (DEPTH, D_MODEL, 2 * D_FF), D_MODEL ** -0.5),
        "ffn_conv_w": nrm(ks[15], (DEPTH, FFN_KERNEL, D_FF), FFN_KERNEL ** -0.5),
        "ffn_conv_b": nrm(ks[16], (DEPTH, D_FF), 0.01),
        "w_down": nrm(ks[17], (DEPTH, D_FF, D_MODEL), D_FF ** -0.5),
        "final_norm_g": 1.0 + nrm(ks[18], (D_MODEL,), 0.02),
    }


def reference(x, c, lb_table, w_ada, b_ada, norm1_g, w_in, hgrn_norm_g, conv_w, conv_b,
              conv_norm_g, conv_norm_b, w_out, norm2_g, w_gu, ffn_conv_w, ffn_conv_b,
              w_down, final_norm_g):
    B, T, _ = x.shape
    dt = x.dtype
    lb_all = jnp.cumsum(jax.nn.softmax(lb_table.astype(jnp.float32), axis=0), axis=0)
    c_act = jax.nn.silu(c)
    h = x
    for l in range(DEPTH):
        mod = (c_act @ w_ada[l] + b_ada[l]).astype(dt)
        sh1, sc1, g1, sh2, sc2, g2 = [m[:, None, :] for m in jnp.split(mod, N_MOD, axis=-1)]

        u = rms_norm(h, norm1_g[l]) * (1 + sc1) + sh1
        z = u @ w_in[l]
        zq, zf, zi, zg, za, zb = jnp.split(z, 6, axis=-1)
        hs = lambda a: a.reshape(B, T, HG_HEADS, HG_HEAD_K)
        lb = lb_all[l].reshape(HG_HEADS, HG_HEAD_K)
        o = hgrn2_recurrence(hs(zq), hs(zf), zi.reshape(B, T, HG_HEADS, HG_HEAD_V), lb)
        o = o * lax.rsqrt(jnp.mean(o * o, axis=-1, keepdims=True) + EPS)
        o = o * hgrn_norm_g[l].astype(jnp.float32)
        o = o.reshape(B, T, HG_WIDTH).astype(dt) * jax.nn.silu(zg)
        v = za * jax.nn.sigmoid(zb)
        v = causal_dwconv(v, conv_w[l], conv_b[l])
        v32 = v.astype(jnp.float32).reshape(B, T, CV_GROUPS, CV_WIDTH // CV_GROUPS)
        mu = jnp.mean(v32, axis=-1, keepdims=True)
        var = jnp.mean(jnp.square(v32 - mu), axis=-1, keepdims=True)
        v32 = ((v32 - mu) * lax.rsqrt(var + EPS)).reshape(B, T, CV_WIDTH)
        v = (v32 * conv_norm_g[l] + conv_norm_b[l]).astype(dt)
        v = jax.nn.silu(v)
        mix = jnp.concatenate([o, v], axis=-1) @ w_out[l]
        h = h + g1 * mix

        u = rms_norm(h, norm2_g[l]) * (1 + sc2) + sh2
        gate, val = jnp.split(u @ w_gu[l], 2, axis=-1)
        gate = causal_dwconv(gate, ffn_conv_w[l], ffn_conv_b[l])
        y = (jax.nn.gelu(gate, approximate=False) * val) @ w_down[l]
        h = h + g2 * y
    return rms_norm(h, final_norm_g)
```

```python
import contextlib
import numpy as np
import concourse.bass as bass
import concourse.mybir as mybir
from concourse.bass_utils import run_bass_kernel_spmd

F32 = mybir.dt.float32
BF16 = mybir.dt.bfloat16
F32R = mybir.dt.float32r
AF = mybir.ActivationFunctionType
ALU = mybir.AluOpType

P = 128
D = 1024
NT = 512
HGW = 512
DFF = 2752
NFT = 22
EPS = 1e-6
NCORES = 8


class T:
    __slots__ = ("ap", "name", "w", "r", "open", "last_pe")

    def __init__(self, ap, name=""):
        self.ap = ap
        self.name = name
        self.w = []
        self.r = []
        self.open = None
        self.last_pe = None

    def __getitem__(self, k):
        return self.ap[k]


class Op:
    __slots__ = ("eng", "fn", "deps", "odeps", "signals", "seq", "dma_key", "cost", "lat", "idx", "epoch",
                 "succ", "prio", "fin", "nd", "rt", "tag", "st", "why", "func", "opname")

    def __init__(self, eng, fn, dma_key=None):
        self.eng = eng
        self.fn = fn
        self.deps = set()
        self.odeps = set()
        self.signals = False
        self.seq = None
        self.dma_key = dma_key
        self.cost = 0.0
        self.lat = 0.0
        self.succ = []
        self.prio = 0.0
        self.fin = None
        self.nd = 0
        self.rt = 0.0


ENGS = ("pe", "act", "dve", "pool", "sp")
LAT_SAME = 60.0
LAT_X = 180.0


class _Probe:
    def __init__(self):
        self.name = None
        self.kw = None

    def __getattr__(self, name):
        def f(*a, **kw):
            self.name = name
            self.kw = kw
            self.args = a
            return self
        return f


def _nfree(ap):
    n = 1
    for d in ap.shape[1:]:
        n *= int(d)
    return n


def _is_psum(ap):
    nm = ap.name
    return nm.startswith("pb") or nm.startswith("pf")


def _estimate(eng, fn, dma_key):
    p = _Probe()
    try:
        fn(p)
    except Exception:
        return 500.0, 0.0
    kw = p.kw or {}
    nm = p.name
    _estimate.last = (nm, str(kw.get('func', '')))
    if nm == "dma_start":
        out = kw["out"]
        nbytes = _nfree(out) * int(out.shape[0]) * (2 if "bfloat16" in str(out.dtype) else 4)
        occ = 650.0 if eng == "pool" else 700.0
        return occ, float(nbytes)
    if eng == "pe":
        if nm == "transpose":
            return 110.0, 0.0
        rhs = kw["rhs"]
        n = _nfree(rhs)
        c = max(n, 64) * 0.44 + 22.0
        if "float32r" in str(rhs.dtype):
            c *= 2
        elif "float32" in str(rhs.dtype):
            c *= 4
        return c, 0.0
    out = kw.get("out", None)
    if out is None and p.args:
        out = p.args[0]
    n = _nfree(out) if out is not None else 512
    if eng == "act":
        return n * 0.75 + 260.0, 0.0
    if eng == "dve":
        two = False
        if nm in ("tensor_tensor", "scalar_tensor_tensor"):
            i0, i1 = kw.get("in0"), kw.get("in1")
            two = not (_is_psum(i0) or _is_psum(i1))
        if nm == "tensor_tensor_scan":
            two = True
        return n * (2.08 if two else 1.04) + 120.0, 0.0
    return n * 1.6 + 350.0, 0.0


ACT_SETS = {"Sigmoid": frozenset({2, 21}), "Exp": frozenset({0, 6, 22}), "Ln": frozenset({5, 6}),
            "Sqrt": frozenset({3, 23}), "Silu": frozenset({18}), "Gelu": frozenset({10}),
            "Tanh": frozenset({0, 2, 8, 10, 11, 12, 18, 19, 20})}
TABLE_LOAD = 1300.0
SWITCH_PEN = 4000.0


def _fname(o):
    return o.func.split(".")[-1] if o.func else ""


def _needs_switch(cur, o):
    f = ACT_SETS.get(_fname(o))
    if f is None:
        return False
    return cur is None or not (cur & f)


def _next_set(cur, o):
    f = ACT_SETS.get(_fname(o))
    if f is None:
        return cur
    if cur is None or not (cur & f):
        return f
    return cur & f


class Sched:
    def __init__(self, nc, same_engine_sync=True, reorder=True):
        self.nc = nc
        self.ops = []
        self.streams = {e: [] for e in ENGS}
        self.dma_keys = {}
        self.same_engine_sync = same_engine_sync
        self.reorder = reorder
        self.epoch = 0
        self.epoch_excl = {}

    def op(self, eng, fn, reads=(), writes=(), pwrites=(), dma_key=None, after=()):
        o = Op(eng, fn, dma_key)
        o.idx = len(self.ops)
        o.epoch = self.epoch
        deps = set(after)
        for t in reads:
            deps.update(t.w)
        for t in writes:
            deps.update(t.w)
            deps.update(t.r)
        for t in pwrites:
            if t.r:
                deps.update(t.r)
            elif t.open is not None:
                if t.open.eng == eng:
                    o.odeps.add(t.open)
                else:
                    deps.add(t.open)
        if eng == "pe":
            for t in list(writes) + list(pwrites):
                if t.last_pe is not None:
                    o.odeps.add(t.last_pe)
                t.last_pe = o
        for t in reads:
            t.r.append(o)
        for t in writes:
            t.w = [o]
            t.r = []
            t.open = o
        for t in pwrites:
            if t.r:
                t.w = [o]
                t.r = []
                t.open = o
            else:
                t.w.append(o)
                if t.open is None:
                    t.open = o
        deps.discard(o)
        o.odeps.discard(o)
        for d in deps:
            if d.dma_key is None and dma_key is None and d.eng == eng:
                if eng == "pe" or not self.same_engine_sync:
                    o.odeps.add(d)
                    continue
            o.deps.add(d)
        for d in o.deps:
            d.signals = True
        _estimate.last = ('', '')
        o.cost, o.lat = _estimate(eng, fn, dma_key)
        o.opname, o.func = _estimate.last
        import sys as _sys
        fr = _sys._getframe(1)
        if fr.f_code.co_name == 'dma':
            fr = fr.f_back
        o.tag = fr.f_lineno
        self.ops.append(o)
        return o

    def dma(self, eng, out_t, out_ap, in_t, in_ap, key, partial=False, after=(), **kw):
        def fn(e):
            return e.dma_start(out=out_ap, in_=in_ap, **kw)
        reads = [in_t] if in_t is not None else []
        w = [out_t] if out_t is not None else []
        if partial:
            return self.op(eng, fn, reads=reads, pwrites=w, dma_key=key, after=after)
        return self.op(eng, fn, reads=reads, writes=w, dma_key=key, after=after)

    def barrier(self, exclude=()):
        self.epoch_excl[self.epoch] = set(exclude)
        self.epoch += 1

    def schedule(self):
        ops = self.ops
        for o in ops:
            o.succ = []
        for o in ops:
            for d in o.deps:
                d.succ.append(o)
            for d in o.odeps:
                d.succ.append(o)
        for o in reversed(ops):
            m = 0.0
            for sc in o.succ:
                if sc.prio > m:
                    m = sc.prio
            o.prio = m + o.cost + (o.lat / 250.0 + 2000.0 if o.dma_key is not None else 0.0)
        free_at = {e: 0.0 for e in ENGS}
        dma_free = [0.0]
        cur_set = [None]
        self.n_table_loads = 0
        streams = {e: [] for e in ENGS}
        keys_last = {}
        nep = self.epoch + 1
        by_epoch = [[] for _ in range(nep)]
        for o in ops:
            by_epoch[o.epoch].append(o)
        for ep in range(nep):
            eops = by_epoch[ep]
            if ep > 0:
                excl = self.epoch_excl.get(ep - 1, set())
                lasts = set()
                for e in ENGS:
                    for o in reversed(streams[e]):
                        if o.dma_key is None:
                            lasts.add(o)
                            break
                for k, o in keys_last.items():
                    if k not in excl:
                        lasts.add(o)
                for o in eops:
                    for d in lasts:
                        if d.dma_key is None and o.dma_key is None and d.eng == o.eng:
                            continue
                        o.deps.add(d)
                        d.signals = True
            avail = {e: [] for e in ENGS}
            for o in eops:
                o.nd = 0
                o.rt = 0.0
                for d in list(o.deps) + list(o.odeps):
                    if d.fin is None:
                        o.nd += 1
                    else:
                        lat = LAT_SAME if d.eng == o.eng else LAT_X
                        if d.fin + lat > o.rt:
                            o.rt = d.fin + lat
            for o in eops:
                if o.nd == 0:
                    avail[o.eng].append(o)
            remaining = len(eops)
            eng_ops = {e: [o for o in eops if o.eng == e] for e in ENGS}
            nxt = {e: 0 for e in ENGS}
            while remaining:
                best = None
                for e in ENGS:
                    av = avail[e]
                    if not av:
                        continue
                    tf = free_at[e]
                    import os
                    strict = os.environ.get("STRICT", "").split(",")
                    if self.reorder and e not in strict:
                        cand = None
                        cp = 0.0
                        for o in av:
                            if o.rt <= tf:
                                p_ = o.prio
                                if e == "act" and _needs_switch(cur_set[0], o):
                                    p_ -= SWITCH_PEN
                                if cand is None or p_ > cp:
                                    cand = o
                                    cp = p_
                        if cand is None:
                            for o in av:
                                if cand is None or o.rt < cand.rt or (o.rt == cand.rt and o.prio > cand.prio):
                                    cand = o
                    else:
                        cand = min(av, key=lambda o: o.idx)
                        while nxt[e] < len(eng_ops[e]) and eng_ops[e][nxt[e]].fin is not None:
                            nxt[e] += 1
                        if cand is not eng_ops[e][nxt[e]]:
                            continue
                    st = max(tf, cand.rt)
                    if best is None or st < best[0]:
                        best = (st, e, cand)
                st, e, o = best
                o.st = st
                o.why = 'eng' if free_at[e] >= o.rt else 'dep'
                avail[e].remove(o)
                remaining -= 1
                if o.dma_key is not None:
                    free_at[e] = st + o.cost
                    ds = max(st + o.cost, dma_free[0])
                    xfer = o.lat / 300.0
                    dma_free[0] = ds + xfer
                    o.fin = ds + xfer + 2000.0
                    keys_last[o.dma_key] = o
                else:
                    c_ = o.cost
                    if e == "act":
                        if _needs_switch(cur_set[0], o):
                            c_ += TABLE_LOAD
                            self.n_table_loads += 1
                        cur_set[0] = _next_set(cur_set[0], o)
                    free_at[e] = st + c_
                    o.fin = st + c_
                streams[e].append(o)
                for sc in o.succ:
                    if sc.epoch != ep:
                        continue
                    lat = LAT_SAME if sc.eng == o.eng else LAT_X
                    if o.fin + lat > sc.rt:
                        sc.rt = o.fin + lat
                    sc.nd -= 1
                    if sc.nd == 0:
                        avail[sc.eng].append(sc)
        self.streams = streams
        self.sim_time = max(free_at.values())
        done = set()
        ptr = {e: 0 for e in ENGS}
        prog = True
        while prog:
            prog = False
            for e in ENGS:
                while ptr[e] < len(streams[e]):
                    o = streams[e][ptr[e]]
                    if all((d in done) for d in o.deps) and all((d in done) for d in o.odeps):
                        done.add(o)
                        ptr[e] += 1
                        prog = True
                    else:
                        break
        stuck = {e: (ptr[e], len(streams[e])) for e in ENGS if ptr[e] < len(streams[e])}
        assert not stuck, ("schedule deadlock", stuck)
        for e in ENGS:
            for o in streams[e]:
                if o.dma_key is not None:
                    self.dma_keys.setdefault(o.dma_key, []).append(o)

    def emit(self, final_wait_ops=()):
        nc = self.nc
        self.schedule()
        for e in ENGS:
            c = 0
            for o in self.streams[e]:
                if o.dma_key is None and o.signals:
                    c += 1
                    o.seq = c
        for k, ops in self.dma_keys.items():
            c = 0
            for o in ops:
                c += 16
                o.seq = c
                o.signals = True
        with contextlib.ExitStack() as es:
            esems = {e: es.enter_context(nc.semaphore("s_" + e)) for e in ENGS}
            dsems = {k: es.enter_context(nc.semaphore("d_%s" % (k,))) for k in self.dma_keys}
            block = es.enter_context(nc.Block())

            def semval(d):
                if d.dma_key is not None:
                    return dsems[d.dma_key], d.seq
                return esems[d.eng], d.seq

            def run(e, eng, extra_final=()):
                waited = {}
                for o in self.streams[e]:
                    need = {}
                    for d in o.deps:
                        s, v = semval(d)
                        if v > need.get(id(s), (0, None))[0]:
                            need[id(s)] = (v, s)
                    for sid, (v, s) in need.items():
                        if waited.get(sid, 0) < v:
                            eng.wait_ge(s, v)
                            waited[sid] = v
                    ins = o.fn(eng)
                    if o.signals:
                        if o.dma_key is not None:
                            ins.then_inc(dsems[o.dma_key], 16)
                        else:
                            ins.then_inc(esems[e], 1)
                for d in extra_final:
                    s, v = semval(d)
                    if waited.get(id(s), 0) < v:
                        eng.wait_ge(s, v)
                        waited[id(s)] = v

            @block.sync
            def _(eng):
                run("sp", eng, extra_final=final_wait_ops)

            @block.tensor
            def _(eng):
                run("pe", eng)

            @block.scalar
            def _(eng):
                run("act", eng)

            @block.vector
            def _(eng):
                run("dve", eng)

            @block.gpsimd
            def _(eng):
                run("pool", eng)


class Ring:
    def __init__(self, tiles):
        self.tiles = tiles
        self.i = 0

    def next(self):
        t = self.tiles[self.i % len(self.tiles)]
        self.i += 1
        return t


def build(nseq=2, tseq=2048, dbg=False, same_engine_sync=True, reorder=True):
    nc = bass.Bass("TRN2", target_bir_lowering=False)
    NTOK = nseq * tseq
    NTILE = tseq // NT
    taps = {}

    def din(name, shape, dt=F32):
        return nc.dram_tensor(name, shape, dt, kind="ExternalInput").ap()

    x = din("x", [NTOK, D])
    c_in = din("c", [nseq, D])
    lb_table = din("lb_table", [2, HGW])
    w_ada = din("w_ada", [D, 6 * D])
    b_ada = din("b_ada", [6 * D])
    norm1_g = din("norm1_g", [D])
    w_in = din("w_in", [D, 3072])
    hgrn_norm_g = din("hgrn_norm_g", [128])
    conv_w = din("conv_w", [31, 512])
    conv_b = din("conv_b", [512])
    conv_norm_g = din("conv_norm_g", [512])
    conv_norm_b = din("conv_norm_b", [512])
    w_out = din("w_out", [D, D])
    norm2_g = din("norm2_g", [D])
    w_gu = din("w_gu", [D, 2 * DFF])
    ffn_conv_w = din("ffn_conv_w", [3, DFF])
    ffn_conv_b = din("ffn_conv_b", [DFF])
    w_down = din("w_down", [DFF, D])
    final_norm_g = din("final_norm_g", [D])
    y = nc.dram_tensor("y", [NTOK, D], F32, kind="ExternalOutput").ap()
    h1 = nc.dram_tensor("h1s", [NTOK, D], F32, kind="Internal").ap()
    win_bf = nc.dram_tensor("win_bf", [D, 3072], BF16, kind="Internal").ap()
    wout_bf = nc.dram_tensor("wout_bf", [D, D], BF16, kind="Internal").ap()
    wgu_bf = nc.dram_tensor("wgu_bf", [D, 2 * DFF], BF16, kind="Internal").ap()
    wdn_bf = nc.dram_tensor("wdn_bf", [DFF, D], BF16, kind="Internal").ap()
    t_win = [T(win_bf, "win_bf%d" % g) for g in range(6)]
    t_wout = T(wout_bf, "wout_bf")
    t_wgu = T(wgu_bf, "wgu_bf")
    t_wdn = T(wdn_bf, "wdn_bf")
    t_h1 = T(h1, "h1")

    S = Sched(nc, same_engine_sync=same_engine_sync, reorder=reorder)
    finals = []

    with contextlib.ExitStack() as es:
        NF = 21984
        NB = 62432
        arena_f = es.enter_context(nc.sbuf_tensor("arena_f", [P, NF], F32))
        arena_b = es.enter_context(nc.sbuf_tensor("arena_b", [P, NB], BF16))
        off = {"f": 0, "b": 0}

        def alloc(kind, free, name=""):
            n = int(np.prod(free))
            ar = arena_f if kind == "f" else arena_b
            lim = NF if kind == "f" else NB
            o = off[kind]
            assert o + n <= lim, ("arena overflow", kind, name, o, n, lim)
            off[kind] = o + n + (n % 2)
            ap = ar[:, o:o + n]
            if len(free) == 2:
                ap = ap.rearrange("p (a b) -> p a b", a=free[0])
            elif len(free) == 3:
                ap = ap.rearrange("p (a b c) -> p a b c", a=free[0], b=free[1])
            return T(ap, name)

        def fa(free, name=""):
            return alloc("f", free, name)

        def ba(free, name=""):
            return alloc("b", free, name)

        PB = [T(es.enter_context(nc.psum_tensor("pb%d" % i, [P, 1024], BF16))[:, :], "pb%d" % i) for i in range(2)]
        PFl = [T(es.enter_context(nc.psum_tensor("pf%d" % i, [P, 512], F32))[:, :], "pf%d" % i) for i in range(6)]
        PF = Ring(PFl)

        def tap(name, t, ap, shape, dt=F32):
            if not dbg:
                return
            d = nc.dram_tensor("dbg_" + name, list(shape), dt, kind="ExternalOutput").ap()
            taps[name] = (list(shape), dt)
            finals.append(S.dma("sp", None, d, t, ap, key="dbg_" + name))

        onesf = fa([128], "onesf")
        identf = fa([128], "identf")
        Gmat = fa([128], "Gmat")
        Cmat = fa([128], "Cmat")
        maskA = fa([128], "maskA")
        m512 = fa([512], "m512")
        vecs = fa([384], "vecs")
        identb = ba([128], "identb")
        Gmatb = ba([128], "Gmatb")
        onesmb = ba([128], "onesmb")

        S.op("pool", lambda e: e.memset(onesf.ap, 1.0), writes=[onesf])
        S.op("pool", lambda e: e.affine_select(out=identf.ap, in_=onesf.ap, pattern=[[1, 128]],
                                               compare_op=ALU.is_equal, fill=0.0, base=0, channel_multiplier=-1),
             reads=[onesf], writes=[identf])
        S.op("pool", lambda e: e.tensor_copy(out=identb.ap, in_=identf.ap), reads=[identf], writes=[identb])
        S.op("pool", lambda e: e.memset(Gmat.ap, 0.0), writes=[Gmat])
        S.op("pool", lambda e: e.memset(Gmat[0:64, 0:64], 1.0 / 64), writes=[Gmat])
        S.op("pool", lambda e: e.memset(Gmat[64:128, 64:128], 1.0 / 64), writes=[Gmat])
        S.op("pool", lambda e: e.tensor_copy(out=Gmatb.ap, in_=Gmat.ap), reads=[Gmat], writes=[Gmatb])
        S.op("pool", lambda e: e.tensor_tensor(out=Cmat.ap, in0=identf.ap, in1=Gmat.ap, op=ALU.subtract),
             reads=[identf, Gmat], writes=[Cmat])
        S.op("pool", lambda e: e.memset(onesmb.ap, 1.0 / 128), writes=[onesmb])
        Cmat05 = fa([128], "Cmat05")
        S.op("pool", lambda e: e.tensor_scalar(out=Cmat05.ap, in0=Cmat.ap, scalar1=0.5, scalar2=1.0, op0=ALU.mult,
                                               op1=ALU.mult), reads=[Cmat], writes=[Cmat05])
        S.op("pool", lambda e: e.affine_select(out=maskA.ap, in_=onesf.ap, pattern=[[1, 128]],
                                               compare_op=ALU.is_ge, fill=0.0, base=0, channel_multiplier=-1),
             reads=[onesf], writes=[maskA])
        S.op("pool", lambda e: e.memset(maskA[0:64, 64:128], 0.0), writes=[maskA])
        S.op("pool", lambda e: e.memset(m512.ap, 1.0), writes=[m512])
        S.op("pool", lambda e: e.memset(m512.ap.rearrange("p (c t) -> p c t", t=64)[:, :, 0:1], 0.0), writes=[m512])

        prev_ops = []
        order = (5, 4, 1, 0, 2, 3)
        for gi, g in enumerate(order):
            cur = []
            for rb in range(8):
                cur.append(S.dma("pool", t_win[g], win_bf[rb * 128:(rb + 1) * 128, g * 512:(g + 1) * 512], None,
                                 w_in[rb * 128:(rb + 1) * 128, g * 512:(g + 1) * 512], key="c_win%d" % g,
                                 partial=True, after=prev_ops[1] if gi >= 4 else ()))
            prev_ops.append(cur)
        for rb in range(8):
            S.dma("pool", t_wout, wout_bf[rb * 128:(rb + 1) * 128, :], None, w_out[rb * 128:(rb + 1) * 128, :],
                  key="c_wout", partial=True, after=prev_ops[3])

        stg = [fa([128], "stg%d" % i) for i in range(3)]
        stg_ms = []
        for i in range(3):
            stg_ms.append(S.op("dve", lambda e, i=i: e.memset(stg[i].ap, 0.0), writes=[stg[i]]))
        N1G, N2G, CB, CNG, CNB, HNG, LB0, LB1, CC, BADA, FCB = 0, 8, 16, 20, 24, 28, 29, 33, 37, 53, 101
        FCW, CW = 128, 256

        def vload(si, row0, nrows, src, ncol=128):
            S.dma("act", stg[si], stg[si][row0:row0 + nrows, 0:ncol], None, src, key="stg%d" % si, partial=True,
                  after=[stg_ms[si]])

        vload(0, N1G, 8, norm1_g.rearrange("(t p) -> t p", p=128))
        vload(0, N2G, 8, norm2_g.rearrange("(t p) -> t p", p=128))
        vload(0, CB, 4, conv_b.rearrange("(t p) -> t p", p=128))
        vload(0, CNG, 4, conv_norm_g.rearrange("(t p) -> t p", p=128))
        vload(0, CNB, 4, conv_norm_b.rearrange("(t p) -> t p", p=128))
        vload(0, HNG, 1, hgrn_norm_g.rearrange("(t p) -> t p", p=128))
        vload(0, LB0, 8, lb_table.rearrange("r (t p) -> (r t) p", p=128))
        vload(0, CC, nseq * 8, c_in.rearrange("b (t p) -> (b t) p", p=128))
        vload(0, BADA, 48, b_ada.rearrange("(t p) -> t p", p=128))
        vload(0, FCB, 21, ffn_conv_b[0:2688].rearrange("(t p) -> t p", p=128))
        vload(0, FCB + 21, 1, ffn_conv_b[2688:2752].rearrange("(t p) -> t p", p=64), ncol=64)
        for k in range(3):
            vload(1, k * 22, 21, ffn_conv_w[k, 0:2688].rearrange("(t p) -> t p", p=128))
            vload(1, k * 22 + 21, 1, ffn_conv_w[k, 2688:2752].rearrange("(t p) -> t p", p=64), ncol=64)
        vload(2, 0, 124, conv_w.rearrange("k (t p) -> (k t) p", p=128))
        for i in range(3):
            pf = PF.next()
            S.op("pe", lambda e, i=i, pf=pf: e.transpose(out=pf[:, 0:128], in_=stg[i].ap, identity=identf.ap),
                 reads=[stg[i], identf], writes=[pf])
            S.op("act", lambda e, i=i, pf=pf: e.activation(out=vecs[:, i * 128:(i + 1) * 128], in_=pf[:, 0:128],
                                                            func=AF.Copy), reads=[pf], pwrites=[vecs])

        def vc(col):
            return vecs[:, col:col + 1]

        lbv = fa([4], "lb")
        oml = fa([4], "oml")
        noml = fa([4], "noml")
        S.op("dve", lambda e: e.tensor_tensor(out=lbv.ap, in0=vecs[:, LB0:LB0 + 4], in1=vecs[:, LB1:LB1 + 4],
                                              op=ALU.subtract), reads=[vecs], writes=[lbv])
        S.op("act", lambda e: e.activation(out=lbv.ap, in_=lbv.ap, func=AF.Sigmoid), reads=[lbv], writes=[lbv])
        S.op("dve", lambda e: e.tensor_scalar(out=oml.ap, in0=lbv.ap, scalar1=-1.0, scalar2=1.0, op0=ALU.mult,
                                              op1=ALU.add), reads=[lbv], writes=[oml])
        S.op("dve", lambda e: e.tensor_scalar(out=noml.ap, in0=lbv.ap, scalar1=-1.0, scalar2=None, op0=ALU.add),
             reads=[lbv], writes=[noml])
        epsT = fa([1], "eps")
        S.op("dve", lambda e: e.memset(epsT.ap, EPS), writes=[epsT])
        neghalf = fa([1], "neghalf")
        S.op("dve", lambda e: e.memset(neghalf.ap, -0.5), writes=[neghalf])
        a1 = fa([4], "a1")
        a2 = fa([4], "a2")
        a3 = fa([4], "a3")
        S.op("dve", lambda e: e.tensor_scalar(out=a1.ap, in0=oml.ap, scalar1=0.5, scalar2=None, op0=ALU.mult),
             reads=[oml], writes=[a1])
        S.op("dve", lambda e: e.tensor_tensor(out=a2.ap, in0=lbv.ap, in1=a1.ap, op=ALU.add), reads=[lbv, a1],
             writes=[a2])
        S.op("dve", lambda e: e.tensor_scalar(out=a3.ap, in0=oml.ap, scalar1=-0.5, scalar2=None, op0=ALU.mult),
             reads=[oml], writes=[a3])

        bcv = fa([4], "bcv")
        pf = PF.next()
        S.op("pe", lambda e, pf=pf: e.matmul(pf[:, 0:4], lhsT=Cmat.ap, rhs=vecs[:, CB:CB + 4], start=True, stop=True),
             reads=[Cmat, vecs], writes=[pf])
        S.op("act", lambda e, pf=pf: e.activation(out=bcv.ap, in_=pf[:, 0:4], func=AF.Copy), reads=[pf], writes=[bcv])

        cact = fa([nseq * 8], "cact")
        S.op("act", lambda e: e.activation(out=cact.ap, in_=vecs[:, CC:CC + nseq * 8], func=AF.Silu),
             reads=[vecs], writes=[cact])
        modv_all = fa([6, nseq, 8], "modv")
        modv = [T(modv_all[:, j, :, :], "modv%d" % j) for j in range(6)]
        scp_all = fa([2, nseq, 8], "scp")
        scp = [T(scp_all[:, w_, :, :], "scp%d" % w_) for w_ in range(2)]
        gbc = {}
        persist_f, persist_b = off["f"], off["b"]
        for b in range(nseq):
            gbc[("g1", b)] = fa([1024], "g1bc%d" % b)

        WA = Ring([fa([8, 128], "wa%d" % i) for i in range(4)])
        dgA = [fa([128], "dg%d" % i) for i in range(2)]
        cactb = ba([nseq * 8], "cactb")
        S.op("dve", lambda e: e.tensor_copy(out=cactb.ap, in_=cact.ap), reads=[cact], writes=[cactb])
        cact_v = cactb.ap.rearrange("p (b k) -> p b k", k=8)
        WB = Ring([ba([8, 128], "wb%d" % i) for i in range(2)])
        wbk = {"k": 0}
        dgk = {"k": 0}

        def adaln(js):
            mp = PF.next()
            for jl, j in enumerate(js):
                for dt in range(8):
                    nt = j * 8 + dt
                    wj = WA.next()
                    S.dma("sp", wj, wj.ap, None,
                          w_ada[:, nt * 128:(nt + 1) * 128].rearrange("(k p) n -> p k n", p=128), key=wj.name)
                    col = (jl * 8 + dt) * nseq
                    wb = WB.next()
                    wbk["k"] += 1
                    if wbk["k"] % 2 == 0:
                        S.op("dve", lambda e, wj=wj, wb=wb: e.tensor_copy(out=wb.ap, in_=wj.ap), reads=[wj], writes=[wb])
                    else:
                        S.op("act", lambda e, wj=wj, wb=wb: e.activation(
                            out=wb.ap.rearrange("p k n -> p (k n)"), in_=wj.ap.rearrange("p k n -> p (k n)"),
                            func=AF.Copy), reads=[wj], writes=[wb])
                    for kt in range(8):
                        S.op("pe", lambda e, wb=wb, kt=kt, col=col, mp=mp: e.matmul(
                            mp[:, col:col + nseq], lhsT=wb[:, kt, :], rhs=cact_v[:, :, kt], start=(kt == 0),
                            stop=(kt == 7)), reads=[wb, cactb], pwrites=[mp])
                S.op("dve", lambda e, j=j, jl=jl, mp=mp: e.tensor_tensor(
                    out=modv[j].ap.rearrange("p b d -> p d b"),
                    in0=mp[:, jl * 8 * nseq:(jl + 1) * 8 * nseq].rearrange("p (d b) -> p d b", d=8),
                    in1=vecs[:, BADA + j * 8:BADA + j * 8 + 8].unsqueeze(2).broadcast_to([P, 8, nseq]),
                    op=ALU.add), reads=[mp, vecs], writes=[modv[j]])

        def make_scp(which, jsc, gcol):
            for b in range(nseq):
                S.op("dve", lambda e, which=which, jsc=jsc, gcol=gcol, b=b: e.scalar_tensor_tensor(
                    out=scp[which][:, b, :], in0=modv[jsc][:, b, :], scalar=1.0, in1=vecs[:, gcol:gcol + 8],
                    op0=ALU.add, op1=ALU.mult), reads=[modv[jsc], vecs], pwrites=[scp[which]])

        def make_gbc(nm, j, dg=None):
            dg = dg or dgA
            for b in range(nseq):
                gt = gbc[(nm, b)]
                for half in range(2):
                    pf = PF.next()
                    for d4 in range(4):
                        dt = half * 4 + d4
                        dgt = dg[dgk["k"] % 2]
                        dgk["k"] += 1
                        S.op("dve", lambda e, dgt=dgt, j=j, b=b, dt=dt: e.tensor_scalar(
                            out=dgt.ap, in0=identf.ap, scalar1=modv[j][:, b, dt:dt + 1], scalar2=None, op0=ALU.mult),
                            reads=[identf, modv[j]], writes=[dgt])
                        S.op("pe", lambda e, pf=pf, dgt=dgt, d4=d4: e.matmul(
                            pf[:, d4 * 128:(d4 + 1) * 128], lhsT=onesf.ap, rhs=dgt.ap, start=True, stop=True),
                            reads=[onesf, dgt], pwrites=[pf])
                    S.op("act", lambda e, pf=pf, gt=gt, half=half: e.activation(
                        out=gt[:, half * 512:(half + 1) * 512], in_=pf.ap, func=AF.Copy), reads=[pf], pwrites=[gt])

        adaln([0, 1])
        make_scp(0, 1, N1G)
        adaln([2])
        make_gbc("g1", 2)
        tap("modv", modv[0], modv_all.ap, [P, 6, nseq, 8])
        if dbg:
            tap("g1bc", gbc[("g1", 0)], gbc[("g1", 0)].ap, [P, 1024])

        cm = ba([124, 128], "cm")
        for t in range(4):
            for k in range(31):
                idx = t * 31 + k
                if idx % 2 == 0:
                    S.op("dve", lambda e, idx=idx, t=t, k=k: e.tensor_scalar(
                        out=cm[:, idx, :], in0=Cmat.ap, scalar1=vc(CW + k * 4 + t), scalar2=0.5, op0=ALU.mult,
                        op1=ALU.mult), reads=[Cmat, vecs], pwrites=[cm])
                else:
                    S.op("act", lambda e, idx=idx, t=t, k=k: e.activation(
                        out=cm[:, idx, :], in_=Cmat05.ap, func=AF.Copy, scale=vc(CW + k * 4 + t)),
                        reads=[Cmat05, vecs], pwrites=[cm])

        ffn_casts = []
        for rb in range(8):
            for ch in range(4):
                ffn_casts.append((t_wgu, wgu_bf[rb * 128:(rb + 1) * 128, ch * 1376:(ch + 1) * 1376],
                                  w_gu[rb * 128:(rb + 1) * 128, ch * 1376:(ch + 1) * 1376], "c_wgu"))
        for rb in range(NFT):
            r1 = min(DFF, (rb + 1) * 128)
            ffn_casts.append((t_wdn, wdn_bf[rb * 128:r1, :], w_down[rb * 128:r1, :], "c_wdn"))

        XR = Ring([fa([1024], "xr%d" % i) for i in range(2)])
        XO = Ring([fa([1024], "xo%d" % i) for i in range(3)])
        PFh = Ring(PFl[0:4])
        PFc = Ring(PFl[4:6])
        xn = ba([4, 1024], "xn")
        uT = ba([8, 512], "uT")
        WI = Ring([ba([8, 512], "wi%d" % i) for i in range(2)])
        RF = Ring([fa([512], "rf%d" % i) for i in range(7)])
        RFc = Ring([fa([512], "rfc%d" % i) for i in range(3)])
        RBF = Ring([ba([512], "rb%d" % i) for i in range(2)])
        RBFc = Ring([ba([512], "rbc%d" % i) for i in range(2)])
        E4 = fa([4, 512], "E4")
        sigb4 = ba([4, 512], "sigb4")
        KTt = ba([4, 512], "KTt")
        QT = ba([4, 512], "QT")
        Ktok = ba([4, 4, 128], "Ktok")
        Vsb = ba([4, 512], "Vsb")
        Abf = ba([4, 4, 128], "Abf")
        vglu = [ba([542], "vglu%d" % t) for t in range(4)]
        gz = ba([4, 512], "gz")
        ovT = ba([8, 512], "ovT")
        Sbf = [ba([4, 128], "Sbf%d" % i) for i in range(9)]
        Sf = fa([4, 128], "Sf")
        tmpS = fa([4, 128], "tmpS")
        eb = fa([8, 4], "eb")
        ssq = Ring([fa([4], "ssq%d" % i) for i in range(2)])
        xk = {"i": 0}

        def rms_rows(src_t, junk_ap, junk_t):
            sst = ssq.next()
            S.op("act", lambda e: e.activation(out=junk_ap, in_=src_t.ap, func=AF.Square, accum_out=sst[:, 0:1]),
                 reads=[src_t], writes=[sst], pwrites=[junk_t])
            S.op("pool", lambda e: e.tensor_scalar(out=sst[:, 1:2], in0=sst[:, 0:1], scalar1=1.0 / D, scalar2=EPS,
                                                   op0=ALU.mult, op1=ALU.add), reads=[sst], writes=[sst])
            S.op("pool", lambda e: e.tensor_tensor(out=sst[:, 2:3], in0=sst[:, 1:2], in1=neghalf.ap, op=ALU.pow),
                 reads=[sst, neghalf], writes=[sst])
            return sst

        def norm_transposes(load_src, r0, scw, jsh, b):
            mark = None
            for j in range(4):
                xt = XR.next()
                o_ = S.dma("sp", xt, xt.ap, load_src[0], load_src[1][r0 + j * 128:r0 + (j + 1) * 128, :],
                           key=xt.name)
                if mark is None:
                    mark = o_
                sst = rms_rows(xt, xn[:, j, :], xn)
                S.op("dve", lambda e, xt=xt, sst=sst, j=j: e.tensor_scalar(
                    out=xn[:, j, :], in0=xt.ap, scalar1=sst[:, 2:3], scalar2=None, op0=ALU.mult),
                    reads=[xt, sst], pwrites=[xn])
            for dt in range(8):
                pb = PB[dt % 2]
                for j in range(4):
                    S.op("pe", lambda e, pb=pb, j=j, dt=dt: e.transpose(
                        out=pb[:, j * 128:(j + 1) * 128], in_=xn[:, j, dt * 128:(dt + 1) * 128], identity=identb.ap),
                        reads=[xn, identb], pwrites=[pb])
                if dt % 2 == 0:
                    S.op("dve", lambda e, pb=pb, dt=dt: e.tensor_scalar(
                        out=uT[:, dt, :], in0=pb[:, 0:512], scalar1=scp[scw][:, b, dt:dt + 1],
                        scalar2=modv[jsh][:, b, dt:dt + 1], op0=ALU.mult, op1=ALU.add),
                        reads=[pb, scp[scw], modv[jsh]], pwrites=[uT])
                else:
                    S.op("act", lambda e, pb=pb, dt=dt: e.activation(
                        out=uT[:, dt, :], in_=pb[:, 0:512], func=AF.Identity, scale=scp[scw][:, b, dt:dt + 1],
                        bias=modv[jsh][:, b, dt:dt + 1]), reads=[pb, scp[scw], modv[jsh]], pwrites=[uT])
            return mark

        def load_w(ring, src_t, src_ap, col0, ncols=512):
            slot = ring.next()
            S.dma("sp", slot, slot[:, :, 0:ncols],
                  src_t, src_ap[:, col0:col0 + ncols].rearrange("(k p) n -> p k n", p=128), key=slot.name)
            return slot

        for s in range(nseq):
            b = s
            S.op("pool", lambda e: e.memset(Sf.ap, 0.0), writes=[Sf])
            S.op("pool", lambda e: e.memset(Sbf[0].ap, 0.0), writes=[Sbf[0]])
            for t in range(4):
                S.op("pool", lambda e, t=t: e.memset(vglu[t][:, 0:30], 0.0), writes=[vglu[t]])
            for i in range(NTILE):
                r0 = s * tseq + i * NT
                first = (s == 0 and i == 0)
                mark = norm_transposes((None, x), r0, 0, 0, b)
                ti = s * NTILE + i
                ntt = nseq * NTILE
                import os
                if ti >= 1 and not os.environ.get('NO_CAST'):
                    lo = (ti - 1) * len(ffn_casts) // (ntt - 1)
                    hi = ti * len(ffn_casts) // (ntt - 1)
                    for (tt_, oap_, iap_, key_) in ffn_casts[lo:hi]:
                        S.dma("pool", tt_, oap_, None, iap_, key=key_, partial=True, after=[mark])
                if first:
                    tap("uT", uT, uT.ap, [P, 8, 512], BF16)
                slot = load_w(WI, t_win[5], win_bf, 2560)
                for t in range(4):
                    Z = PFc.next()
                    for kt in range(8):
                        S.op("pe", lambda e, Z=Z, slot=slot, kt=kt, t=t: e.matmul(
                            Z.ap, lhsT=slot[:, kt, t * 128:(t + 1) * 128], rhs=uT[:, kt, :], start=(kt == 0),
                            stop=(kt == 7)), reads=[slot, uT], writes=[Z] if kt == 0 else [], pwrites=[] if kt == 0 else [Z])
                    S.op("act", lambda e, Z=Z, t=t: e.activation(out=sigb4[:, t, :], in_=Z.ap, func=AF.Tanh, scale=0.5),
                         reads=[Z], pwrites=[sigb4])
                slot = load_w(WI, t_win[4], win_bf, 2048)
                for t in range(4):
                    Z = PFc.next()
                    for kt in range(8):
                        S.op("pe", lambda e, Z=Z, slot=slot, kt=kt, t=t: e.matmul(
                            Z.ap, lhsT=slot[:, kt, t * 128:(t + 1) * 128], rhs=uT[:, kt, :], start=(kt == 0),
                            stop=(kt == 7)), reads=[slot, uT], writes=[Z] if kt == 0 else [], pwrites=[] if kt == 0 else [Z])
                    vg = vglu[t]
                    S.op("dve", lambda e, Z=Z, t=t, vg=vg: e.scalar_tensor_tensor(
                        out=vg[:, 30:542], in0=sigb4[:, t, :], scalar=1.0, in1=Z.ap, op0=ALU.add, op1=ALU.mult),
                         reads=[Z, sigb4], writes=[vg])
                for t in range(4):
                    vg = vglu[t]
                    Dp = PFc.next()
                    for k in range(31):
                        S.op("pe", lambda e, Dp=Dp, t=t, k=k, vg=vg: e.matmul(
                            Dp.ap, lhsT=cm[:, t * 31 + k, :], rhs=vg[:, k:k + 512], start=(k == 0), stop=(k == 30)),
                            reads=[cm, vg], writes=[Dp] if k == 0 else [], pwrites=[] if k == 0 else [Dp])
                    S.op("pool", lambda e, vg=vg: e.tensor_copy(out=vg[:, 0:30], in_=vg[:, 512:542]),
                         reads=[vg], writes=[vg])
                    dsb = RFc.next()
                    S.op("act", lambda e, Dp=Dp, dsb=dsb, t=t: e.activation(
                        out=dsb.ap, in_=Dp.ap, func=AF.Identity, bias=bcv[:, t:t + 1]), reads=[Dp, bcv], writes=[dsb])
                    dsq = RBFc.next()
                    S.op("act", lambda e, Dp=Dp, dsq=dsq, t=t: e.activation(
                        out=dsq.ap, in_=Dp.ap, func=AF.Square, bias=bcv[:, t:t + 1]), reads=[Dp, bcv], writes=[dsq])
                    Vp = PFc.next()
                    S.op("pe", lambda e, Vp=Vp, dsq=dsq: e.matmul(Vp.ap, lhsT=Gmatb.ap, rhs=dsq.ap, start=True,
                                                                  stop=True), reads=[Gmatb, dsq], writes=[Vp])
                    rc = RFc.next()
                    S.op("act", lambda e, Vp=Vp, rc=rc: e.activation(out=rc.ap, in_=Vp.ap, func=AF.Ln,
                                                                    bias=epsT.ap), reads=[Vp, epsT], writes=[rc])
                    S.op("act", lambda e, rc=rc: e.activation(out=rc.ap, in_=rc.ap, func=AF.Exp, scale=-0.5),
                         reads=[rc], writes=[rc])
                    S.op("pool", lambda e, rc=rc, dsb=dsb: e.tensor_tensor(out=dsb.ap, in0=dsb.ap, in1=rc.ap,
                                                                          op=ALU.mult), reads=[rc, dsb], writes=[dsb])
                    S.op("act", lambda e, dsb=dsb, t=t: e.activation(
                        out=ovT[:, 4 + t, :], in_=dsb.ap, func=AF.Silu, scale=vc(CNG + t), bias=vc(CNB + t)),
                        reads=[dsb, vecs], pwrites=[ovT])
                slot = load_w(WI, t_win[1], win_bf, 512)
                for h in range(4):
                    Z = PFh.next()
                    for kt in range(8):
                        S.op("pe", lambda e, Z=Z, slot=slot, kt=kt, h=h: e.matmul(
                            Z.ap, lhsT=slot[:, kt, h * 128:(h + 1) * 128], rhs=uT[:, kt, :], start=(kt == 0),
                            stop=(kt == 7)), reads=[slot, uT], writes=[Z] if kt == 0 else [], pwrites=[] if kt == 0 else [Z])
                    sig = RF.next()
                    S.op("act", lambda e, Z=Z, sig=sig: e.activation(out=sig.ap, in_=Z.ap, func=AF.Tanh, scale=0.5),
                         reads=[Z], writes=[sig])
                    logf = RF.next()
                    S.op("act", lambda e, sig=sig, logf=logf, h=h: e.activation(
                        out=logf.ap, in_=sig.ap, func=AF.Ln, scale=a1[:, h:h + 1], bias=a2[:, h:h + 1]),
                        reads=[sig, a1, a2], writes=[logf])
                    bcum = RF.next()
                    S.op("dve", lambda e, bcum=bcum, logf=logf: e.tensor_tensor_scan(
                        out=bcum.ap, data0=m512.ap, data1=logf.ap, initial=0.0, op0=ALU.mult, op1=ALU.add),
                        reads=[m512, logf], writes=[bcum])
                    kk = RF.next()
                    S.op("pool", lambda e, kk=kk, sig=sig, h=h: e.tensor_scalar(
                        out=kk.ap, in0=sig.ap, scalar1=a3[:, h:h + 1], scalar2=a1[:, h:h + 1], op0=ALU.mult,
                        op1=ALU.add), reads=[sig, a3, a1], writes=[kk])
                    S.op("act", lambda e, bcum=bcum, h=h: e.activation(out=E4[:, h, :], in_=bcum.ap, func=AF.Exp),
                         reads=[bcum], pwrites=[E4])
                    einv = RF.next()
                    S.op("act", lambda e, bcum=bcum, einv=einv: e.activation(out=einv.ap, in_=bcum.ap, func=AF.Exp,
                                                                            scale=-1.0), reads=[bcum], writes=[einv])
                    S.op("pool", lambda e, h=h: e.tensor_copy(
                        out=eb[:, :, h], in_=E4[:, h, :].rearrange("p (c t) -> p c t", t=64)[:, :, 63]),
                        reads=[E4], pwrites=[eb])
                    S.op("pool", lambda e, kk=kk, einv=einv, h=h: e.tensor_tensor(
                        out=KTt[:, h, :], in0=kk.ap, in1=einv.ap, op=ALU.mult), reads=[kk, einv], pwrites=[KTt])
                    if first and h == 0:
                        tap("bcum0", bcum, bcum.ap, [P, 512])
                slot = load_w(WI, t_win[0], win_bf, 0)
                for h in range(4):
                    Z = PFh.next()
                    for kt in range(8):
                        S.op("pe", lambda e, Z=Z, slot=slot, kt=kt, h=h: e.matmul(
                            Z.ap, lhsT=slot[:, kt, h * 128:(h + 1) * 128], rhs=uT[:, kt, :], start=(kt == 0),
                            stop=(kt == 7)), reads=[slot, uT], writes=[Z] if kt == 0 else [], pwrites=[] if kt == 0 else [Z])
                    S.op("dve", lambda e, Z=Z, h=h: e.tensor_tensor(out=QT[:, h, :], in0=Z.ap, in1=E4[:, h, :],
                                                                    op=ALU.mult), reads=[Z, E4], pwrites=[QT])
                slot = load_w(WI, t_win[2], win_bf, 1024)
                for j in range(4):
                    Z = PFh.next()
                    for kt in range(8):
                        S.op("pe", lambda e, Z=Z, slot=slot, kt=kt, j=j: e.matmul(
                            Z.ap, lhsT=uT[:, kt, j * 128:(j + 1) * 128], rhs=slot[:, kt, :], start=(kt == 0),
                            stop=(kt == 7)), reads=[slot, uT], writes=[Z] if kt == 0 else [], pwrites=[] if kt == 0 else [Z])
                    S.op("dve", lambda e, Z=Z, j=j: e.tensor_copy(out=Vsb[:, j, :], in_=Z.ap),
                         reads=[Z], pwrites=[Vsb])
                slot = load_w(WI, t_win[3], win_bf, 1536)
                for h in range(4):
                    Z = PFh.next()
                    for kt in range(8):
                        S.op("pe", lambda e, Z=Z, slot=slot, kt=kt, h=h: e.matmul(
                            Z.ap, lhsT=slot[:, kt, h * 128:(h + 1) * 128], rhs=uT[:, kt, :], start=(kt == 0),
                            stop=(kt == 7)), reads=[slot, uT], writes=[Z] if kt == 0 else [], pwrites=[] if kt == 0 else [Z])
                    sg = RF.next()
                    S.op("act", lambda e, Z=Z, sg=sg: e.activation(out=sg.ap, in_=Z.ap, func=AF.Silu),
                         reads=[Z], writes=[sg])
                    S.op("pool", lambda e, sg=sg, h=h: e.tensor_scalar(
                        out=gz[:, h, :], in0=sg.ap, scalar1=vc(HNG), scalar2=1.0, op0=ALU.mult, op1=ALU.mult),
                        reads=[sg, vecs], pwrites=[gz])
                for j in range(4):
                    pb = PB[j % 2]
                    for h in range(4):
                        S.op("pe", lambda e, pb=pb, j=j, h=h: e.transpose(
                            out=pb[:, h * 128:(h + 1) * 128], in_=KTt[:, h, j * 128:(j + 1) * 128],
                            identity=identb.ap), reads=[KTt, identb], pwrites=[pb])
                    S.op("dve", lambda e, pb=pb, j=j: e.tensor_copy(
                        out=Ktok[:, j, :, :].rearrange("p h k -> p (h k)"), in_=pb[:, 0:512]),
                        reads=[pb], pwrites=[Ktok])
                if i > 0:
                    S.op("pool", lambda e: e.tensor_copy(out=Sbf[0].ap, in_=Sbf[8].ap), reads=[Sbf[8]],
                         writes=[Sbf[0]])
                for c in range(8):
                    j, pbase = c // 2, (c % 2) * 64
                    DS = PFh.next()
                    for h in range(4):
                        S.op("pe", lambda e, DS=DS, j=j, pbase=pbase, h=h: e.matmul(
                            DS[:, h * 128:(h + 1) * 128], lhsT=Ktok[pbase:pbase + 64, j, h, :],
                            rhs=Vsb[pbase:pbase + 64, j, h * 128:(h + 1) * 128], start=True, stop=True),
                            reads=[Ktok, Vsb], pwrites=[DS])
                    S.op("dve", lambda e, DS=DS: e.tensor_tensor(
                        out=tmpS.ap, in0=DS.ap.rearrange("p (h v) -> p h v", h=4), in1=Sf.ap, op=ALU.add),
                        reads=[DS, Sf], writes=[tmpS])
                    S.op("dve", lambda e, c=c: e.tensor_tensor(
                        out=Sf.ap, in0=tmpS.ap, in1=eb[:, c, :].unsqueeze(2).broadcast_to([P, 4, 128]), op=ALU.mult),
                        reads=[tmpS, eb], writes=[Sf])
                    S.op("act", lambda e, c=c: e.activation(
                        out=Sbf[c + 1].ap.rearrange("p h v -> p (h v)"), in_=Sf.ap.rearrange("p h v -> p (h v)"),
                        func=AF.Copy), reads=[Sf], writes=[Sbf[c + 1]])
                for h in range(4):
                    A = PFh.next()
                    for j in range(4):
                        S.op("pe", lambda e, A=A, j=j, h=h: e.matmul(
                            A[:, j * 128:(j + 1) * 128], lhsT=KTt[:, h, j * 128:(j + 1) * 128],
                            rhs=QT[:, h, j * 128:(j + 1) * 128], start=True, stop=True),
                            reads=[KTt, QT], pwrites=[A])
                    S.op("dve", lambda e, A=A, h=h: e.tensor_tensor(
                        out=Abf[:, h, :, :], in0=A.ap.rearrange("p (j t) -> p j t", j=4),
                        in1=maskA.ap.unsqueeze(1).broadcast_to([P, 4, 128]), op=ALU.mult),
                        reads=[A, maskA], pwrites=[Abf])
                for j in range(4):
                    O = PFh.next()
                    for h in range(4):
                        S.op("pe", lambda e, O=O, j=j, h=h: e.matmul(
                            O[:, h * 128:(h + 1) * 128], lhsT=Vsb[:, j, h * 128:(h + 1) * 128], rhs=Abf[:, h, j, :],
                            start=True, stop=False), reads=[Vsb, Abf], pwrites=[O])
                        for cc in range(2):
                            c = 2 * j + cc
                            S.op("pe", lambda e, O=O, j=j, h=h, cc=cc, c=c: e.matmul(
                                O[:, h * 128 + cc * 64:h * 128 + cc * 64 + 64], lhsT=Sbf[c][:, h, :],
                                rhs=QT[:, h, j * 128 + cc * 64:j * 128 + cc * 64 + 64], start=False, stop=(cc == 1)),
                                reads=[Sbf[c], QT], pwrites=[O])
                    osb = RF.next()
                    S.op("act", lambda e, O=O, osb=osb: e.activation(out=osb.ap, in_=O.ap, func=AF.Copy),
                         reads=[O], writes=[osb])
                    osq = RBF.next()
                    S.op("act", lambda e, O=O, osq=osq: e.activation(out=osq.ap, in_=O.ap, func=AF.Square),
                         reads=[O], writes=[osq])
                    MS = PFh.next()
                    S.op("pe", lambda e, MS=MS, osq=osq: e.matmul(MS.ap, lhsT=onesmb.ap, rhs=osq.ap, start=True,
                                                                  stop=True), reads=[onesmb, osq], writes=[MS])
                    ro = RF.next()
                    S.op("act", lambda e, MS=MS, ro=ro: e.activation(out=ro.ap, in_=MS.ap, func=AF.Ln,
                                                                    bias=epsT.ap), reads=[MS, epsT], writes=[ro])
                    S.op("act", lambda e, ro=ro: e.activation(out=ro.ap, in_=ro.ap, func=AF.Exp, scale=-0.5),
                         reads=[ro], writes=[ro])
                    S.op("pool", lambda e, ro=ro, osb=osb: e.tensor_tensor(out=osb.ap, in0=osb.ap, in1=ro.ap,
                                                                          op=ALU.mult), reads=[ro, osb], writes=[osb])
                    S.op("pool", lambda e, osb=osb, j=j: e.tensor_tensor(
                        out=ovT[:, 0:4, j * 128:(j + 1) * 128], in0=osb.ap.rearrange("p (h t) -> p h t", h=4),
                        in1=gz[:, :, j * 128:(j + 1) * 128], op=ALU.mult), reads=[osb, gz], pwrites=[ovT])
                if first:
                    tap("ovT", ovT, ovT.ap, [P, 8, 512], BF16)
                wo = [load_w(WI, t_wout, wout_bf, half * 512) for half in range(2)]
                g1t = gbc[("g1", b)]
                for j in range(4):
                    xr = XO.next()
                    S.dma("sp", xr, xr.ap, None, x[r0 + j * 128:r0 + (j + 1) * 128, :], key=xr.name)
                    for half in range(2):
                        slot = wo[half]
                        M = PFh.next()
                        for kt in range(8):
                            S.op("pe", lambda e, M=M, slot=slot, kt=kt, j=j: e.matmul(
                                M.ap, lhsT=ovT[:, kt, j * 128:(j + 1) * 128], rhs=slot[:, kt, :], start=(kt == 0),
                                stop=(kt == 7)), reads=[slot, ovT], writes=[M] if kt == 0 else [],
                                pwrites=[] if kt == 0 else [M])
                        tmp = RF.next()
                        S.op("dve", lambda e, M=M, tmp=tmp, half=half, g1t=g1t: e.tensor_tensor(
                            out=tmp.ap, in0=M.ap, in1=g1t[:, half * 512:(half + 1) * 512], op=ALU.mult),
                            reads=[M, g1t], writes=[tmp])
                        S.op("pool", lambda e, tmp=tmp, xr=xr, half=half: e.tensor_tensor(
                            out=xr[:, half * 512:(half + 1) * 512], in0=xr[:, half * 512:(half + 1) * 512],
                            in1=tmp.ap, op=ALU.add), reads=[tmp, xr], writes=[xr])
                    S.dma("sp", t_h1, h1[r0 + j * 128:r0 + (j + 1) * 128, :], xr, xr.ap, key="h1st_" + xr.name, partial=True)
                    if first and j == 0:
                        tap("h1_0", xr, xr.ap, [P, 1024])
                import os
                if first and not os.environ.get('NO_LATE'):
                    adaln([3, 4])
                    make_scp(1, 4, N2G)
                    adaln([5])
        S.barrier()

        off["f"], off["b"] = persist_f, persist_b
        for b in range(nseq):
            gbc[("g2", b)] = fa([1024], "g2bc%d" % b)
        fgbc = fa([1024], "fgbc")
        S.dma("sp", fgbc, fgbc.ap, None, final_norm_g.partition_broadcast(P), key="fgbc")
        dgB = [fa([128], "dgb%d" % i) for i in range(2)]
        make_gbc("g2", 5, dgB)
        XR2 = Ring([fa([1024], "xq%d" % i) for i in range(3)])
        XO2 = Ring([fa([1024], "xp%d" % i) for i in range(2)])
        PFg = Ring(PFl[0:2])
        PFv = Ring(PFl[2:4])
        PFy = Ring(PFl[4:6])
        xn2 = ba([4, 1024], "xn2")
        uT2 = ba([8, 512], "u2T")
        wdn = ba([NFT, 1024], "wdn")
        WG = Ring([ba([8, 512], "wg%d" % i) for i in range(3)])
        actT = ba([NFT, 512], "actT")
        GB = Ring([fa([514], "gb%d" % i) for i in range(3)])
        RF2 = Ring([fa([512], "rg%d" % i) for i in range(8)])
        ghist = [fa([2], "ghist%d" % i) for i in range(NFT)]
        ssq2 = Ring([fa([4], "ssr%d" % i) for i in range(4)])
        ssqF = Ring([fa([4], "ssf%d" % i) for i in range(4)])
        YO = Ring([fa([1024], "yo%d" % i) for i in range(2)])
        for kt in range(NFT):
            r1 = min(DFF, (kt + 1) * 128)
            S.dma("sp", wdn, wdn[0:r1 - kt * 128, kt, :], t_wdn, wdn_bf[kt * 128:r1, :], key="wdn", partial=True)

        for s in range(nseq):
            b = s
            for ft in range(NFT):
                S.op("pool", lambda e, ft=ft: e.memset(ghist[ft].ap, 0.0), writes=[ghist[ft]])
            for i in range(NTILE):
                r0 = s * tseq + i * NT
                first = (s == 0 and i == 0)
                hts = []
                for j in range(4):
                    xt = XR2.next()
                    S.dma("sp", xt, xt.ap, t_h1, h1[r0 + j * 128:r0 + (j + 1) * 128, :], key=xt.name)
                    hts.append(xt)
                    sst = ssq2.next()
                    S.op("act", lambda e, xt=xt, sst=sst, j=j: e.activation(out=xn2[:, j, :], in_=xt.ap, func=AF.Square,
                                                                             accum_out=sst[:, 0:1]),
                         reads=[xt], writes=[sst], pwrites=[xn2])
                    S.op("pool", lambda e, sst=sst: e.tensor_scalar(out=sst[:, 1:2], in0=sst[:, 0:1], scalar1=1.0 / D,
                                                                    scalar2=EPS, op0=ALU.mult, op1=ALU.add),
                         reads=[sst], writes=[sst])
                    S.op("pool", lambda e, sst=sst: e.tensor_tensor(out=sst[:, 2:3], in0=sst[:, 1:2], in1=neghalf.ap,
                                                                    op=ALU.pow), reads=[sst, neghalf], writes=[sst])
                    S.op("dve", lambda e, xt=xt, sst=sst, j=j: e.tensor_scalar(
                        out=xn2[:, j, :], in0=xt.ap, scalar1=sst[:, 2:3], scalar2=None, op0=ALU.mult),
                        reads=[xt, sst], pwrites=[xn2])
                for dt in range(8):
                    pb = PB[dt % 2]
                    for j in range(4):
                        S.op("pe", lambda e, pb=pb, j=j, dt=dt: e.transpose(
                            out=pb[:, j * 128:(j + 1) * 128], in_=xn2[:, j, dt * 128:(dt + 1) * 128],
                            identity=identb.ap), reads=[xn2, identb], pwrites=[pb])
                    if dt % 2 == 0:
                        S.op("dve", lambda e, pb=pb, dt=dt, b=b: e.tensor_scalar(
                            out=uT2[:, dt, :], in0=pb[:, 0:512], scalar1=scp[1][:, b, dt:dt + 1],
                            scalar2=modv[3][:, b, dt:dt + 1], op0=ALU.mult, op1=ALU.add),
                            reads=[pb, scp[1], modv[3]], pwrites=[uT2])
                    else:
                        S.op("act", lambda e, pb=pb, dt=dt, b=b: e.activation(
                            out=uT2[:, dt, :], in_=pb[:, 0:512], func=AF.Identity, scale=scp[1][:, b, dt:dt + 1],
                            bias=modv[3][:, b, dt:dt + 1]), reads=[pb, scp[1], modv[3]], pwrites=[uT2])
                for g2 in range(11):
                    ncol = 256 if g2 < 10 else 192
                    slot = WG.next()
                    S.dma("sp", slot, slot[:, :, 0:ncol], t_wgu,
                          wgu_bf[:, g2 * 256:g2 * 256 + ncol].rearrange("(k p) n -> p k n", p=128), key=slot.name,
                          partial=True)
                    S.dma("sp", slot, slot[:, :, 256:256 + ncol], t_wgu,
                          wgu_bf[:, DFF + g2 * 256:DFF + g2 * 256 + ncol].rearrange("(k p) n -> p k n", p=128),
                          key=slot.name, partial=True)
                    for sub in range(2):
                        ft = g2 * 2 + sub
                        if ft >= NFT:
                            continue
                        m = 128 if ft < NFT - 1 else 64
                        G = PFg.next()
                        Vv = PFv.next()
                        for kt in range(8):
                            S.op("pe", lambda e, G=G, slot=slot, kt=kt, sub=sub, m=m: e.matmul(
                                G[0:m, :], lhsT=slot[:, kt, sub * 128:sub * 128 + m], rhs=uT2[:, kt, :],
                                start=(kt == 0), stop=(kt == 7)), reads=[slot, uT2],
                                writes=[G] if kt == 0 else [], pwrites=[] if kt == 0 else [G])
                        for kt in range(8):
                            S.op("pe", lambda e, Vv=Vv, slot=slot, kt=kt, sub=sub, m=m: e.matmul(
                                Vv[0:m, :], lhsT=slot[:, kt, 256 + sub * 128:256 + sub * 128 + m], rhs=uT2[:, kt, :],
                                start=(kt == 0), stop=(kt == 7)), reads=[slot, uT2],
                                writes=[Vv] if kt == 0 else [], pwrites=[] if kt == 0 else [Vv])
                        gb = GB.next()
                        acc = RF2.next()
                        S.op("act", lambda e, G=G, acc=acc, ft=ft, m=m: e.activation(
                            out=acc[0:m, :], in_=G[0:m, :], func=AF.Identity, scale=vecs[0:m, FCW + 44 + ft:FCW + 44 + ft + 1],
                            bias=vecs[0:m, FCB + ft:FCB + ft + 1]), reads=[G, vecs], writes=[acc])
                        S.op("pool", lambda e, gb=gb, ft=ft, m=m: e.tensor_copy(out=gb[0:m, 0:2], in_=ghist[ft][0:m, :]),
                             reads=[ghist[ft]], writes=[gb])
                        S.op("act", lambda e, gb=gb, G=G, m=m: e.activation(out=gb[0:m, 2:514], in_=G[0:m, :],
                                                                           func=AF.Copy), reads=[G], pwrites=[gb])
                        S.op("pool", lambda e, gb=gb, ft=ft, m=m: e.tensor_copy(out=ghist[ft][0:m, :],
                                                                               in_=gb[0:m, 512:514]),
                             reads=[gb], writes=[ghist[ft]])
                        S.op("dve", lambda e, gb=gb, acc=acc, ft=ft, m=m: e.scalar_tensor_tensor(
                            out=acc[0:m, :], in0=gb[0:m, 1:513], scalar=vecs[0:m, FCW + 22 + ft:FCW + 22 + ft + 1],
                            in1=acc[0:m, :], op0=ALU.mult, op1=ALU.add), reads=[gb, acc, vecs], writes=[acc])
                        S.op("dve", lambda e, gb=gb, acc=acc, ft=ft, m=m: e.scalar_tensor_tensor(
                            out=acc[0:m, :], in0=gb[0:m, 0:512], scalar=vecs[0:m, FCW + ft:FCW + ft + 1],
                            in1=acc[0:m, :], op0=ALU.mult, op1=ALU.add), reads=[gb, acc, vecs], writes=[acc])
                        S.op("act", lambda e, acc=acc, m=m: e.activation(out=acc[0:m, :], in_=acc[0:m, :],
                                                                        func=AF.Gelu), reads=[acc], writes=[acc])
                        S.op("dve", lambda e, Vv=Vv, acc=acc, ft=ft, m=m: e.tensor_tensor(
                            out=actT[0:m, ft, :], in0=Vv[0:m, :], in1=acc[0:m, :], op=ALU.mult),
                            reads=[Vv, acc], pwrites=[actT])
                if first:
                    tap("actT", actT, actT.ap, [P, NFT, 512], BF16)
                for j in range(4):
                    ht = XO2.next()
                    S.dma("sp", ht, ht.ap, t_h1, h1[r0 + j * 128:r0 + (j + 1) * 128, :], key=ht.name)
                    for half in range(2):
                        Y = PFy.next()
                        for kt in range(NFT):
                            kk_ = 128 if kt < NFT - 1 else 64
                            S.op("pe", lambda e, Y=Y, kt=kt, j=j, half=half, kk_=kk_: e.matmul(
                                Y.ap, lhsT=actT[0:kk_, kt, j * 128:(j + 1) * 128],
                                rhs=wdn[0:kk_, kt, half * 512:(half + 1) * 512], start=(kt == 0),
                                stop=(kt == NFT - 1)), reads=[actT, wdn], writes=[Y] if kt == 0 else [],
                                pwrites=[] if kt == 0 else [Y])
                        tmp = RF2.next()
                        S.op("dve", lambda e, Y=Y, tmp=tmp, half=half, b=b: e.tensor_tensor(
                            out=tmp.ap, in0=Y.ap, in1=gbc[("g2", b)][:, half * 512:(half + 1) * 512], op=ALU.mult),
                            reads=[Y, gbc[("g2", b)]], writes=[tmp])
                        S.op("pool", lambda e, tmp=tmp, ht=ht, half=half: e.tensor_tensor(
                            out=ht[:, half * 512:(half + 1) * 512], in0=ht[:, half * 512:(half + 1) * 512],
                            in1=tmp.ap, op=ALU.add), reads=[tmp, ht], writes=[ht])
                    sst = ssqF.next()
                    yo = YO.next()
                    S.op("act", lambda e, ht=ht, sst=sst, yo=yo: e.activation(out=yo.ap, in_=ht.ap, func=AF.Square,
                                                                               accum_out=sst[:, 0:1]),
                         reads=[ht], writes=[yo, sst])
                    S.op("pool", lambda e, sst=sst: e.tensor_scalar(out=sst[:, 1:2], in0=sst[:, 0:1], scalar1=1.0 / D,
                                                                    scalar2=EPS, op0=ALU.mult, op1=ALU.add),
                         reads=[sst], writes=[sst])
                    S.op("pool", lambda e, sst=sst: e.tensor_tensor(out=sst[:, 2:3], in0=sst[:, 1:2], in1=neghalf.ap,
                                                                    op=ALU.pow), reads=[sst, neghalf], writes=[sst])
                    S.op("dve", lambda e, ht=ht, sst=sst, yo=yo: e.scalar_tensor_tensor(
                        out=yo.ap, in0=ht.ap, scalar=sst[:, 2:3], in1=fgbc.ap, op0=ALU.mult, op1=ALU.mult),
                        reads=[ht, sst, fgbc], writes=[yo])
                    finals.append(S.dma("sp", None, y[r0 + j * 128:r0 + (j + 1) * 128, :], yo, yo.ap, key=yo.name))
        S.emit(final_wait_ops=finals)
    return nc, taps


def _prep_shared(inputs):
    sq = lambda a: np.ascontiguousarray(np.asarray(a, dtype=np.float32))
    return {
        "lb_table": sq(inputs["lb_table"]),
        "w_ada": sq(inputs["w_ada"][0]),
        "b_ada": sq(inputs["b_ada"][0]),
        "norm1_g": sq(inputs["norm1_g"][0]),
        "w_in": sq(inputs["w_in"][0]),
        "hgrn_norm_g": sq(inputs["hgrn_norm_g"][0]),
        "conv_w": sq(inputs["conv_w"][0]),
        "conv_b": sq(inputs["conv_b"][0]),
        "conv_norm_g": sq(inputs["conv_norm_g"][0]),
        "conv_norm_b": sq(inputs["conv_norm_b"][0]),
        "w_out": sq(inputs["w_out"][0]),
        "norm2_g": sq(inputs["norm2_g"][0]),
        "w_gu": sq(inputs["w_gu"][0]),
        "ffn_conv_w": sq(inputs["ffn_conv_w"][0]),
        "ffn_conv_b": sq(inputs["ffn_conv_b"][0]),
        "w_down": sq(inputs["w_down"][0]),
        "final_norm_g": sq(inputs["final_norm_g"]),
    }


def kernel(**inputs):
    x = np.asarray(inputs["x"], dtype=np.float32)
    c = np.asarray(inputs["c"], dtype=np.float32)
    B, Tn, Dm = x.shape
    nseq = B // NCORES
    nc = build(nseq=nseq, tseq=Tn)[0]
    shared = _prep_shared(inputs)
    in_maps = []
    for i in range(NCORES):
        m = dict(shared)
        m["x"] = np.ascontiguousarray(x[i * nseq:(i + 1) * nseq].reshape(nseq * Tn, Dm))
        m["c"] = np.ascontiguousarray(c[i * nseq:(i + 1) * nseq])
        in_maps.append(m)
    res = run_bass_kernel_spmd(nc, in_maps, core_ids=list(range(NCORES)))
    out = np.concatenate([r["y"].reshape(nseq, Tn, Dm) for r in res.results], axis=0)
    return out.astype(np.float32)
```

```python
import contextlib
import numpy as np
import concourse.bass as bass
import concourse.mybir as mybir
from concourse.bass_utils import run_bass_kernel_spmd

F32 = mybir.dt.float32
BF16 = mybir.dt.bfloat16
F32R = mybir.dt.float32r
AF = mybir.ActivationFunctionType
ALU = mybir.AluOpType

P = 128
D = 1024
NT = 512
HGW = 512
DFF = 2752
NFT = 22
EPS = 1e-6
NCORES = 8


class T:
    __slots__ = ("ap", "name", "w", "r", "open", "last_pe")

    def __init__(self, ap, name=""):
        self.ap = ap
        self.name = name
        self.w = []
        self.r = []
        self.open = None
        self.last_pe = None

    def __getitem__(self, k):
        return self.ap[k]


class Op:
    __slots__ = ("eng", "fn", "deps", "odeps", "signals", "seq", "dma_key", "cost", "lat", "idx", "epoch",
                 "succ", "prio", "fin", "nd", "rt", "tag", "st", "why", "func", "opname")

    def __init__(self, eng, fn, dma_key=None):
        self.eng = eng
        self.fn = fn
        self.deps = set()
        self.odeps = set()
        self.signals = False
        self.seq = None
        self.dma_key = dma_key
        self.cost = 0.0
        self.lat = 0.0
        self.succ = []
        self.prio = 0.0
        self.fin = None
        self.nd = 0
        self.rt = 0.0


ENGS = ("pe", "act", "dve", "pool", "sp")
LAT_SAME = 60.0
LAT_X = 180.0


class _Probe:
    def __init__(self):
        self.name = None
        self.kw = None

    def __getattr__(self, name):
        def f(*a, **kw):
            self.name = name
            self.kw = kw
            self.args = a
            return self
        return f


def _nfree(ap):
    n = 1
    for d in ap.shape[1:]:
        n *= int(d)
    return n


def _is_psum(ap):
    nm = ap.name
    return nm.startswith("pb") or nm.startswith("pf")


def _estimate(eng, fn, dma_key):
    p = _Probe()
    try:
        fn(p)
    except Exception:
        return 500.0, 0.0
    kw = p.kw or {}
    nm = p.name
    _estimate.last = (nm, str(kw.get('func', '')))
    if nm == "dma_start":
        out = kw["out"]
        nbytes = _nfree(out) * int(out.shape[0]) * (2 if "bfloat16" in str(out.dtype) else 4)
        occ = 650.0 if eng == "pool" else 700.0
        return occ, float(nbytes)
    if eng == "pe":
        if nm == "transpose":
            return 110.0, 0.0
        rhs = kw["rhs"]
        n = _nfree(rhs)
        c = max(n, 64) * 0.44 + 22.0
        if "float32r" in str(rhs.dtype):
            c *= 2
        elif "float32" in str(rhs.dtype):
            c *= 4
        return c, 0.0
    out = kw.get("out", None)
    if out is None and p.args:
        out = p.args[0]
    n = _nfree(out) if out is not None else 512
    if eng == "act":
        return n * 0.75 + 260.0, 0.0
    if eng == "dve":
        two = False
        if nm in ("tensor_tensor", "scalar_tensor_tensor"):
            i0, i1 = kw.get("in0"), kw.get("in1")
            two = not (_is_psum(i0) or _is_psum(i1))
        if nm == "tensor_tensor_scan":
            two = True
        return n * (2.08 if two else 1.04) + 120.0, 0.0
    return n * 1.6 + 350.0, 0.0


ACT_SETS = {"Sigmoid": frozenset({2, 21}), "Exp": frozenset({0, 6, 22}), "Ln": frozenset({5, 6}),
            "Sqrt": frozenset({3, 23}), "Silu": frozenset({18}), "Gelu": frozenset({10}),
            "Tanh": frozenset({0, 2, 8, 10, 11, 12, 18, 19, 20})}
TABLE_LOAD = 1300.0
SWITCH_PEN = 4000.0


def _fname(o):
    return o.func.split(".")[-1] if o.func else ""


def _needs_switch(cur, o):
    f = ACT_SETS.get(_fname(o))
    if f is None:
        return False
    return cur is None or not (cur & f)


def _next_set(cur, o):
    f = ACT_SETS.get(_fname(o))
    if f is None:
        return cur
    if cur is None or not (cur & f):
        return f
    return cur & f


class Sched:
    def __init__(self, nc, same_engine_sync=True, reorder=True):
        self.nc = nc
        self.ops = []
        self.streams = {e: [] for e in ENGS}
        self.dma_keys = {}
        self.same_engine_sync = same_engine_sync
        self.reorder = reorder
        self.epoch = 0
        self.epoch_excl = {}

    def op(self, eng, fn, reads=(), writes=(), pwrites=(), dma_key=None, after=()):
        o = Op(eng, fn, dma_key)
        o.idx = len(self.ops)
        o.epoch = self.epoch
        deps = set(after)
        for t in reads:
            deps.update(t.w)
        for t in writes:
            deps.update(t.w)
            deps.update(t.r)
        for t in pwrites:
            if t.r:
                deps.update(t.r)
            elif t.open is not None:
                if t.open.eng == eng:
                    o.odeps.add(t.open)
                else:
                    deps.add(t.open)
        if eng == "pe":
            for t in list(writes) + list(pwrites):
                if t.last_pe is not None:
                    o.odeps.add(t.last_pe)
                t.last_pe = o
        for t in reads:
            t.r.append(o)
        for t in writes:
            t.w = [o]
            t.r = []
            t.open = o
        for t in pwrites:
            if t.r:
                t.w = [o]
                t.r = []
                t.open = o
            else:
                t.w.append(o)
                if t.open is None:
                    t.open = o
        deps.discard(o)
        o.odeps.discard(o)
        for d in deps:
            if d.dma_key is None and dma_key is None and d.eng == eng:
                if eng == "pe" or not self.same_engine_sync:
                    o.odeps.add(d)
                    continue
            o.deps.add(d)
        for d in o.deps:
            d.signals = True
        _estimate.last = ('', '')
        o.cost, o.lat = _estimate(eng, fn, dma_key)
        o.opname, o.func = _estimate.last
        import sys as _sys
        fr = _sys._getframe(1)
        if fr.f_code.co_name == 'dma':
            fr = fr.f_back
        o.tag = fr.f_lineno
        self.ops.append(o)
        return o

    def dma(self, eng, out_t, out_ap, in_t, in_ap, key, partial=False, after=(), **kw):
        def fn(e):
            return e.dma_start(out=out_ap, in_=in_ap, **kw)
        reads = [in_t] if in_t is not None else []
        w = [out_t] if out_t is not None else []
        if partial:
            return self.op(eng, fn, reads=reads, pwrites=w, dma_key=key, after=after)
        return self.op(eng, fn, reads=reads, writes=w, dma_key=key, after=after)

    def barrier(self, exclude=()):
        self.epoch_excl[self.epoch] = set(exclude)
        self.epoch += 1

    def schedule(self):
        ops = self.ops
        for o in ops:
            o.succ = []
        for o in ops:
            for d in o.deps:
                d.succ.append(o)
            for d in o.odeps:
                d.succ.append(o)
        for o in reversed(ops):
            m = 0.0
            for sc in o.succ:
                if sc.prio > m:
                    m = sc.prio
            o.prio = m + o.cost + (o.lat / 250.0 + 2000.0 if o.dma_key is not None else 0.0)
        free_at = {e: 0.0 for e in ENGS}
        dma_free = [0.0]
        cur_set = [None]
        self.n_table_loads = 0
        streams = {e: [] for e in ENGS}
        keys_last = {}
        nep = self.epoch + 1
        by_epoch = [[] for _ in range(nep)]
        for o in ops:
            by_epoch[o.epoch].append(o)
        for ep in range(nep):
            eops = by_epoch[ep]
            if ep > 0:
                excl = self.epoch_excl.get(ep - 1, set())
                lasts = set()
                for e in ENGS:
                    for o in reversed(streams[e]):
                        if o.dma_key is None:
                            lasts.add(o)
                            break
                for k, o in keys_last.items():
                    if k not in excl:
                        lasts.add(o)
                for o in eops:
                    for d in lasts:
                        if d.dma_key is None and o.dma_key is None and d.eng == o.eng:
                            continue
                        o.deps.add(d)
                        d.signals = True
            avail = {e: [] for e in ENGS}
            for o in eops:
                o.nd = 0
                o.rt = 0.0
                for d in list(o.deps) + list(o.odeps):
                    if d.fin is None:
                        o.nd += 1
                    else:
                        lat = LAT_SAME if d.eng == o.eng else LAT_X
                        if d.fin + lat > o.rt:
                            o.rt = d.fin + lat
            for o in eops:
                if o.nd == 0:
                    avail[o.eng].append(o)
            remaining = len(eops)
            eng_ops = {e: [o for o in eops if o.eng == e] for e in ENGS}
            nxt = {e: 0 for e in ENGS}
            while remaining:
                best = None
                for e in ENGS:
                    av = avail[e]
                    if not av:
                        continue
                    tf = free_at[e]
                    import os
                    strict = os.environ.get("STRICT", "").split(",")
                    if self.reorder and e not in strict:
                        cand = None
                        cp = 0.0
                        for o in av:
                            if o.rt <= tf:
                                p_ = o.prio
                                if e == "act" and _needs_switch(cur_set[0], o):
                                    p_ -= SWITCH_PEN
                                if cand is None or p_ > cp:
                                    cand = o
                                    cp = p_
                        if cand is None:
                            for o in av:
                                if cand is None or o.rt < cand.rt or (o.rt == cand.rt and o.prio > cand.prio):
                                    cand = o
                    else:
                        cand = min(av, key=lambda o: o.idx)
                        while nxt[e] < len(eng_ops[e]) and eng_ops[e][nxt[e]].fin is not None:
                            nxt[e] += 1
                        if cand is not eng_ops[e][nxt[e]]:
                            continue
                    st = max(tf, cand.rt)
                    if best is None or st < best[0]:
                        best = (st, e, cand)
                st, e, o = best
                o.st = st
                o.why = 'eng' if free_at[e] >= o.rt else 'dep'
                avail[e].remove(o)
                remaining -= 1
                if o.dma_key is not None:
                    free_at[e] = st + o.cost
                    ds = max(st + o.cost, dma_free[0])
                    xfer = o.lat / 300.0
                    dma_free[0] = ds + xfer
                    o.fin = ds + xfer + 2000.0
                    keys_last[o.dma_key] = o
                else:
                    c_ = o.cost
                    if e == "act":
                        if _needs_switch(cur_set[0], o):
                            c_ += TABLE_LOAD
                            self.n_table_loads += 1
                        cur_set[0] = _next_set(cur_set[0], o)
                    free_at[e] = st + c_
                    o.fin = st + c_
                streams[e].append(o)
                for sc in o.succ:
                    if sc.epoch != ep:
                        continue
                    lat = LAT_SAME if sc.eng == o.eng else LAT_X
                    if o.fin + lat > sc.rt:
                        sc.rt = o.fin + lat
                    sc.nd -= 1
                    if sc.nd == 0:
                        avail[sc.eng].append(sc)
        self.streams = streams
        self.sim_time = max(free_at.values())
        done = set()
        ptr = {e: 0 for e in ENGS}
        prog = True
        while prog:
            prog = False
            for e in ENGS:
                while ptr[e] < len(streams[e]):
                    o = streams[e][ptr[e]]
                    if all((d in done) for d in o.deps) and all((d in done) for d in o.odeps):
                        done.add(o)
                        ptr[e] += 1
                        prog = True
                    else:
                        break
        stuck = {e: (ptr[e], len(streams[e])) for e in ENGS if ptr[e] < len(streams[e])}
        assert not stuck, ("schedule deadlock", stuck)
        for e in ENGS:
            for o in streams[e]:
                if o.dma_key is not None:
                    self.dma_keys.setdefault(o.dma_key, []).append(o)

    def emit(self, final_wait_ops=()):
        nc = self.nc
        self.schedule()
        for e in ENGS:
            c = 0
            for o in self.streams[e]:
                if o.dma_key is None and o.signals:
                    c += 1
                    o.seq = c
        for k, ops in self.dma_keys.items():
            c = 0
            for o in ops:
                c += 16
                o.seq = c
                o.signals = True
        with contextlib.ExitStack() as es:
            esems = {e: es.enter_context(nc.semaphore("s_" + e)) for e in ENGS}
            dsems = {k: es.enter_context(nc.semaphore("d_%s" % (k,))) for k in self.dma_keys}
            block = es.enter_context(nc.Block())

            def semval(d):
                if d.dma_key is not None:
                    return dsems[d.dma_key], d.seq
                return esems[d.eng], d.seq

            def run(e, eng, extra_final=()):
                waited = {}
                for o in self.streams[e]:
                    need = {}
                    for d in o.deps:
                        s, v = semval(d)
                        if v > need.get(id(s), (0, None))[0]:
                            need[id(s)] = (v, s)
                    for sid, (v, s) in need.items():
                        if waited.get(sid, 0) < v:
                            eng.wait_ge(s, v)
                            waited[sid] = v
                    ins = o.fn(eng)
                    if o.signals:
                        if o.dma_key is not None:
                            ins.then_inc(dsems[o.dma_key], 16)
                        else:
                            ins.then_inc(esems[e], 1)
                for d in extra_final:
                    s, v = semval(d)
                    if waited.get(id(s), 0) < v:
                        eng.wait_ge(s, v)
                        waited[id(s)] = v

            @block.sync
            def _(eng):
                run("sp", eng, extra_final=final_wait_ops)

            @block.tensor
            def _(eng):
                run("pe", eng)

            @block.scalar
            def _(eng):
                run("act", eng)

            @block.vector
            def _(eng):
                run("dve", eng)

            @block.gpsimd
            def _(eng):
                run("pool", eng)


class Ring:
    def __init__(self, tiles):
        self.tiles = tiles
        self.i = 0

    def next(self):
        t = self.tiles[self.i % len(self.tiles)]
        self.i += 1
        return t


def build(nseq=2, tseq=2048, dbg=False, same_engine_sync=True, reorder=True):
    nc = bass.Bass("TRN2", target_bir_lowering=False)
    NTOK = nseq * tseq
    NTILE = tseq // NT
    taps = {}

    def din(name, shape, dt=F32):
        return nc.dram_tensor(name, shape, dt, kind="ExternalInput").ap()

    x = din("x", [NTOK, D])
    c_in = din("c", [nseq, D])
    lb_table = din("lb_table", [2, HGW])
    w_ada = din("w_ada", [D, 6 * D])
    b_ada = din("b_ada", [6 * D])
    norm1_g = din("norm1_g", [D])
    w_in = din("w_in", [D, 3072])
    hgrn_norm_g = din("hgrn_norm_g", [128])
    conv_w = din("conv_w", [31, 512])
    conv_b = din("conv_b", [512])
    conv_norm_g = din("conv_norm_g", [512])
    conv_norm_b = din("conv_norm_b", [512])
    w_out = din("w_out", [D, D])
    norm2_g = din("norm2_g", [D])
    w_gu = din("w_gu", [D, 2 * DFF])
    ffn_conv_w = din("ffn_conv_w", [3, DFF])
    ffn_conv_b = din("ffn_conv_b", [DFF])
    w_down = din("w_down", [DFF, D])
    final_norm_g = din("final_norm_g", [D])
    y = nc.dram_tensor("y", [NTOK, D], F32, kind="ExternalOutput").ap()
    h1 = nc.dram_tensor("h1s", [NTOK, D], F32, kind="Internal").ap()
    win_bf = nc.dram_tensor("win_bf", [D, 3072], BF16, kind="Internal").ap()
    wout_bf = nc.dram_tensor("wout_bf", [D, D], BF16, kind="Internal").ap()
    wgu_bf = nc.dram_tensor("wgu_bf", [D, 2 * DFF], BF16, kind="Internal").ap()
    wdn_bf = nc.dram_tensor("wdn_bf", [DFF, D], BF16, kind="Internal").ap()
    t_win = [T(win_bf, "win_bf%d" % g) for g in range(6)]
    t_wout = T(wout_bf, "wout_bf")
    t_wgu = T(wgu_bf, "wgu_bf")
    t_wdn = T(wdn_bf, "wdn_bf")
    t_h1 = T(h1, "h1")

    S = Sched(nc, same_engine_sync=same_engine_sync, reorder=reorder)
    finals = []

    with contextlib.ExitStack() as es:
        NF = 21984
        NB = 62432
        arena_f = es.enter_context(nc.sbuf_tensor("arena_f", [P, NF], F32))
        arena_b = es.enter_context(nc.sbuf_tensor("arena_b", [P, NB], BF16))
        off = {"f": 0, "b": 0}

        def alloc(kind, free, name=""):
            n = int(np.prod(free))
            ar = arena_f if kind == "f" else arena_b
            lim = NF if kind == "f" else NB
            o = off[kind]
            assert o + n <= lim, ("arena overflow", kind, name, o, n, lim)
            off[kind] = o + n + (n % 2)
            ap = ar[:, o:o + n]
            if len(free) == 2:
                ap = ap.rearrange("p (a b) -> p a b", a=free[0])
            elif len(free) == 3:
                ap = ap.rearrange("p (a b c) -> p a b c", a=free[0], b=free[1])
            return T(ap, name)

        def fa(free, name=""):
            return alloc("f", free, name)

        def ba(free, name=""):
            return alloc("b", free, name)

        PB = [T(es.enter_context(nc.psum_tensor("pb%d" % i, [P, 1024], BF16))[:, :], "pb%d" % i) for i in range(2)]
        PFl = [T(es.enter_context(nc.psum_tensor("pf%d" % i, [P, 512], F32))[:, :], "pf%d" % i) for i in range(6)]
        PF = Ring(PFl)

        def tap(name, t, ap, shape, dt=F32):
            if not dbg:
                return
            d = nc.dram_tensor("dbg_" + name, list(shape), dt, kind="ExternalOutput").ap()
            taps[name] = (list(shape), dt)
            finals.append(S.dma("sp", None, d, t, ap, key="dbg_" + name))

        onesf = fa([128], "onesf")
        identf = fa([128], "identf")
        Gmat = fa([128], "Gmat")
        Cmat = fa([128], "Cmat")
        maskA = fa([128], "maskA")
        m512 = fa([512], "m512")
        vecs = fa([384], "vecs")
        identb = ba([128], "identb")
        Gmatb = ba([128], "Gmatb")
        onesmb = ba([128], "onesmb")

        S.op("pool", lambda e: e.memset(onesf.ap, 1.0), writes=[onesf])
        S.op("pool", lambda e: e.affine_select(out=identf.ap, in_=onesf.ap, pattern=[[1, 128]],
                                               compare_op=ALU.is_equal, fill=0.0, base=0, channel_multiplier=-1),
             reads=[onesf], writes=[identf])
        S.op("pool", lambda e: e.tensor_copy(out=identb.ap, in_=identf.ap), reads=[identf], writes=[identb])
        S.op("pool", lambda e: e.memset(Gmat.ap, 0.0), writes=[Gmat])
        S.op("pool", lambda e: e.memset(Gmat[0:64, 0:64], 1.0 / 64), writes=[Gmat])
        S.op("pool", lambda e: e.memset(Gmat[64:128, 64:128], 1.0 / 64), writes=[Gmat])
        S.op("pool", lambda e: e.tensor_copy(out=Gmatb.ap, in_=Gmat.ap), reads=[Gmat], writes=[Gmatb])
        S.op("pool", lambda e: e.tensor_tensor(out=Cmat.ap, in0=identf.ap, in1=Gmat.ap, op=ALU.subtract),
             reads=[identf, Gmat], writes=[Cmat])
        S.op("pool", lambda e: e.memset(onesmb.ap, 1.0 / 128), writes=[onesmb])
        Cmat05 = fa([128], "Cmat05")
        S.op("pool", lambda e: e.tensor_scalar(out=Cmat05.ap, in0=Cmat.ap, scalar1=0.5, scalar2=1.0, op0=ALU.mult,
                                               op1=ALU.mult), reads=[Cmat], writes=[Cmat05])
        S.op("pool", lambda e: e.affine_select(out=maskA.ap, in_=onesf.ap, pattern=[[1, 128]],
                                               compare_op=ALU.is_ge, fill=0.0, base=0, channel_multiplier=-1),
             reads=[onesf], writes=[maskA])
        S.op("pool", lambda e: e.memset(maskA[0:64, 64:128], 0.0), writes=[maskA])
        S.op("pool", lambda e: e.memset(m512.ap, 1.0), writes=[m512])
        S.op("pool", lambda e: e.memset(m512.ap.rearrange("p (c t) -> p c t", t=64)[:, :, 0:1], 0.0), writes=[m512])

        prev_ops = []
        order = (5, 4, 1, 0, 2, 3)
        for gi, g in enumerate(order):
            cur = []
            for rb in range(8):
                cur.append(S.dma("pool", t_win[g], win_bf[rb * 128:(rb + 1) * 128, g * 512:(g + 1) * 512], None,
                                 w_in[rb * 128:(rb + 1) * 128, g * 512:(g + 1) * 512], key="c_win%d" % g,
                                 partial=True, after=prev_ops[1] if gi >= 4 else ()))
            prev_ops.append(cur)
        for rb in range(8):
            S.dma("pool", t_wout, wout_bf[rb * 128:(rb + 1) * 128, :], None, w_out[rb * 128:(rb + 1) * 128, :],
                  key="c_wout", partial=True, after=prev_ops[3])

        stg = [fa([128], "stg%d" % i) for i in range(3)]
        stg_ms = []
        for i in range(3):
            stg_ms.append(S.op("dve", lambda e, i=i: e.memset(stg[i].ap, 0.0), writes=[stg[i]]))
        N1G, N2G, CB, CNG, CNB, HNG, LB0, LB1, CC, BADA, FCB = 0, 8, 16, 20, 24, 28, 29, 33, 37, 53, 101
        FCW, CW = 128, 256

        def vload(si, row0, nrows, src, ncol=128):
            S.dma("act", stg[si], stg[si][row0:row0 + nrows, 0:ncol], None, src, key="stg%d" % si, partial=True,
                  after=[stg_ms[si]])

        vload(0, N1G, 8, norm1_g.rearrange("(t p) -> t p", p=128))
        vload(0, N2G, 8, norm2_g.rearrange("(t p) -> t p", p=128))
        vload(0, CB, 4, conv_b.rearrange("(t p) -> t p", p=128))
        vload(0, CNG, 4, conv_norm_g.rearrange("(t p) -> t p", p=128))
        vload(0, CNB, 4, conv_norm_b.rearrange("(t p) -> t p", p=128))
        vload(0, HNG, 1, hgrn_norm_g.rearrange("(t p) -> t p", p=128))
        vload(0, LB0, 8, lb_table.rearrange("r (t p) -> (r t) p", p=128))
        vload(0, CC, nseq * 8, c_in.rearrange("b (t p) -> (b t) p", p=128))
        vload(0, BADA, 48, b_ada.rearrange("(t p) -> t p", p=128))
        vload(0, FCB, 21, ffn_conv_b[0:2688].rearrange("(t p) -> t p", p=128))
        vload(0, FCB + 21, 1, ffn_conv_b[2688:2752].rearrange("(t p) -> t p", p=64), ncol=64)
        for k in range(3):
            vload(1, k * 22, 21, ffn_conv_w[k, 0:2688].rearrange("(t p) -> t p", p=128))
            vload(1, k * 22 + 21, 1, ffn_conv_w[k, 2688:2752].rearrange("(t p) -> t p", p=64), ncol=64)
        vload(2, 0, 124, conv_w.rearrange("k (t p) -> (k t) p", p=128))
        for i in range(3):
            pf = PF.next()
            S.op("pe", lambda e, i=i, pf=pf: e.transpose(out=pf[:, 0:128], in_=stg[i].ap, identity=identf.ap),
                 reads=[stg[i], identf], writes=[pf])
            S.op("act", lambda e, i=i, pf=pf: e.activation(out=vecs[:, i * 128:(i + 1) * 128], in_=pf[:, 0:128],
                                                            func=AF.Copy), reads=[pf], pwrites=[vecs])

        def vc(col):
            return vecs[:, col:col + 1]

        lbv = fa([4], "lb")
        oml = fa([4], "oml")
        noml = fa([4], "noml")
        S.op("dve", lambda e: e.tensor_tensor(out=lbv.ap, in0=vecs[:, LB0:LB0 + 4], in1=vecs[:, LB1:LB1 + 4],
                                              op=ALU.subtract), reads=[vecs], writes=[lbv])
        S.op("act", lambda e: e.activation(out=lbv.ap, in_=lbv.ap, func=AF.Sigmoid), reads=[lbv], writes=[lbv])
        S.op("dve", lambda e: e.tensor_scalar(out=oml.ap, in0=lbv.ap, scalar1=-1.0, scalar2=1.0, op0=ALU.mult,
                                              op1=ALU.add), reads=[lbv], writes=[oml])
        S.op("dve", lambda e: e.tensor_scalar(out=noml.ap, in0=lbv.ap, scalar1=-1.0, scalar2=None, op0=ALU.add),
             reads=[lbv], writes=[noml])
        epsT = fa([1], "eps")
        S.op("dve", lambda e: e.memset(epsT.ap, EPS), writes=[epsT])
        neghalf = fa([1], "neghalf")
        S.op("dve", lambda e: e.memset(neghalf.ap, -0.5), writes=[neghalf])
        a1 = fa([4], "a1")
        a2 = fa([4], "a2")
        a3 = fa([4], "a3")
        S.op("dve", lambda e: e.tensor_scalar(out=a1.ap, in0=oml.ap, scalar1=0.5, scalar2=None, op0=ALU.mult),
             reads=[oml], writes=[a1])
        S.op("dve", lambda e: e.tensor_tensor(out=a2.ap, in0=lbv.ap, in1=a1.ap, op=ALU.add), reads=[lbv, a1],
             writes=[a2])
        S.op("dve", lambda e: e.tensor_scalar(out=a3.ap, in0=oml.ap, scalar1=-0.5, scalar2=None, op0=ALU.mult),
             reads=[oml], writes=[a3])

        bcv = fa([4], "bcv")
        pf = PF.next()
        S.op("pe", lambda e, pf=pf: e.matmul(pf[:, 0:4], lhsT=Cmat.ap, rhs=vecs[:, CB:CB + 4], start=True, stop=True),
             reads=[Cmat, vecs], writes=[pf])
        S.op("act", lambda e, pf=pf: e.activation(out=bcv.ap, in_=pf[:, 0:4], func=AF.Copy), reads=[pf], writes=[bcv])

        cact = fa([nseq * 8], "cact")
        S.op("act", lambda e: e.activation(out=cact.ap, in_=vecs[:, CC:CC + nseq * 8], func=AF.Silu),
             reads=[vecs], writes=[cact])
        modv_all = fa([6, nseq, 8], "modv")
        modv = [T(modv_all[:, j, :, :], "modv%d" % j) for j in range(6)]
        scp_all = fa([2, nseq, 8], "scp")
        scp = [T(scp_all[:, w_, :, :], "scp%d" % w_) for w_ in range(2)]
        gbc = {}
        persist_f, persist_b = off["f"], off["b"]
        for b in range(nseq):
            gbc[("g1", b)] = fa([1024], "g1bc%d" % b)

        WA = Ring([fa([8, 128], "wa%d" % i) for i in range(4)])
        dgA = [fa([128], "dg%d" % i) for i in range(2)]
        cactb = ba([nseq * 8], "cactb")
        S.op("dve", lambda e: e.tensor_copy(out=cactb.ap, in_=cact.ap), reads=[cact], writes=[cactb])
        cact_v = cactb.ap.rearrange("p (b k) -> p b k", k=8)
        WB = Ring([ba([8, 128], "wb%d" % i) for i in range(2)])
        wbk = {"k": 0}
        dgk = {"k": 0}

        def adaln(js, after=()):
            mp = PF.next()
            for jl, j in enumerate(js):
                for dt in range(8):
                    nt = j * 8 + dt
                    wj = WA.next()
                    S.dma("sp", wj, wj.ap, None,
                          w_ada[:, nt * 128:(nt + 1) * 128].rearrange("(k p) n -> p k n", p=128), key=wj.name,
                          after=after)
                    col = (jl * 8 + dt) * nseq
                    wb = WB.next()
                    wbk["k"] += 1
                    if wbk["k"] % 2 == 0:
                        S.op("dve", lambda e, wj=wj, wb=wb: e.tensor_copy(out=wb.ap, in_=wj.ap), reads=[wj], writes=[wb])
                    else:
                        S.op("act", lambda e, wj=wj, wb=wb: e.activation(
                            out=wb.ap.rearrange("p k n -> p (k n)"), in_=wj.ap.rearrange("p k n -> p (k n)"),
                            func=AF.Copy), reads=[wj], writes=[wb])
                    for kt in range(8):
                        S.op("pe", lambda e, wb=wb, kt=kt, col=col, mp=mp: e.matmul(
                            mp[:, col:col + nseq], lhsT=wb[:, kt, :], rhs=cact_v[:, :, kt], start=(kt == 0),
                            stop=(kt == 7)), reads=[wb, cactb], pwrites=[mp])
                S.op("dve", lambda e, j=j, jl=jl, mp=mp: e.tensor_tensor(
                    out=modv[j].ap.rearrange("p b d -> p d b"),
                    in0=mp[:, jl * 8 * nseq:(jl + 1) * 8 * nseq].rearrange("p (d b) -> p d b", d=8),
                    in1=vecs[:, BADA + j * 8:BADA + j * 8 + 8].unsqueeze(2).broadcast_to([P, 8, nseq]),
                    op=ALU.add), reads=[mp, vecs], writes=[modv[j]])

        def make_scp(which, jsc, gcol):
            for b in range(nseq):
                S.op("dve", lambda e, which=which, jsc=jsc, gcol=gcol, b=b: e.scalar_tensor_tensor(
                    out=scp[which][:, b, :], in0=modv[jsc][:, b, :], scalar=1.0, in1=vecs[:, gcol:gcol + 8],
                    op0=ALU.add, op1=ALU.mult), reads=[modv[jsc], vecs], pwrites=[scp[which]])

        def make_gbc(nm, j, dg=None):
            dg = dg or dgA
            for b in range(nseq):
                gt = gbc[(nm, b)]
                for half in range(2):
                    pf = PF.next()
                    for d4 in range(4):
                        dt = half * 4 + d4
                        dgt = dg[dgk["k"] % 2]
                        dgk["k"] += 1
                        S.op("dve", lambda e, dgt=dgt, j=j, b=b, dt=dt: e.tensor_scalar(
                            out=dgt.ap, in0=identf.ap, scalar1=modv[j][:, b, dt:dt + 1], scalar2=None, op0=ALU.mult),
                            reads=[identf, modv[j]], writes=[dgt])
                        S.op("pe", lambda e, pf=pf, dgt=dgt, d4=d4: e.matmul(
                            pf[:, d4 * 128:(d4 + 1) * 128], lhsT=onesf.ap, rhs=dgt.ap, start=True, stop=True),
                            reads=[onesf, dgt], pwrites=[pf])
                    S.op("act", lambda e, pf=pf, gt=gt, half=half: e.activation(
                        out=gt[:, half * 512:(half + 1) * 512], in_=pf.ap, func=AF.Copy), reads=[pf], pwrites=[gt])

        adaln([0, 1])
        make_scp(0, 1, N1G)
        adaln([2])
        make_gbc("g1", 2)
        tap("modv", modv[0], modv_all.ap, [P, 6, nseq, 8])
        if dbg:
            tap("g1bc", gbc[("g1", 0)], gbc[("g1", 0)].ap, [P, 1024])

        cm = ba([124, 128], "cm")
        for t in range(4):
            for k in range(31):
                idx = t * 31 + k
                if idx % 2 == 0:
                    S.op("dve", lambda e, idx=idx, t=t, k=k: e.tensor_scalar(
                        out=cm[:, idx, :], in0=Cmat.ap, scalar1=vc(CW + k * 4 + t), scalar2=0.5, op0=ALU.mult,
                        op1=ALU.mult), reads=[Cmat, vecs], pwrites=[cm])
                else:
                    S.op("act", lambda e, idx=idx, t=t, k=k: e.activation(
                        out=cm[:, idx, :], in_=Cmat05.ap, func=AF.Copy, scale=vc(CW + k * 4 + t)),
                        reads=[Cmat05, vecs], pwrites=[cm])

        ffn_casts = []
        for rb in range(8):
            for ch in range(4):
                ffn_casts.append((t_wgu, wgu_bf[rb * 128:(rb + 1) * 128, ch * 1376:(ch + 1) * 1376],
                                  w_gu[rb * 128:(rb + 1) * 128, ch * 1376:(ch + 1) * 1376], "c_wgu"))
        for rb in range(NFT):
            r1 = min(DFF, (rb + 1) * 128)
            ffn_casts.append((t_wdn, wdn_bf[rb * 128:r1, :], w_down[rb * 128:r1, :], "c_wdn"))

        XR = Ring([fa([1024], "xr%d" % i) for i in range(2)])
        XO = Ring([fa([1024], "xo%d" % i) for i in range(3)])
        PFh = Ring(PFl[0:4])
        PFc = Ring(PFl[4:6])
        xn = ba([4, 1024], "xn")
        uT = ba([8, 512], "uT")
        WI = Ring([ba([8, 512], "wi%d" % i) for i in range(2)])
        RF = Ring([fa([512], "rf%d" % i) for i in range(7)])
        RFc = Ring([fa([512], "rfc%d" % i) for i in range(3)])
        RBF = Ring([ba([512], "rb%d" % i) for i in range(2)])
        RBFc = Ring([ba([512], "rbc%d" % i) for i in range(2)])
        E4 = fa([4, 512], "E4")
        sigb4 = ba([4, 512], "sigb4")
        KTt = ba([4, 512], "KTt")
        QT = ba([4, 512], "QT")
        Ktok = ba([4, 4, 128], "Ktok")
        Vsb = ba([4, 512], "Vsb")
        Abf = ba([4, 4, 128], "Abf")
        vglu = [ba([542], "vglu%d" % t) for t in range(4)]
        gz = ba([4, 512], "gz")
        ovT = ba([8, 512], "ovT")
        Sbf = [ba([4, 128], "Sbf%d" % i) for i in range(9)]
        Sf = fa([4, 128], "Sf")
        tmpS = fa([4, 128], "tmpS")
        eb = fa([8, 4], "eb")
        ssq = Ring([fa([4], "ssq%d" % i) for i in range(2)])
        xk = {"i": 0}

        def rms_rows(src_t, junk_ap, junk_t):
            sst = ssq.next()
            S.op("act", lambda e: e.activation(out=junk_ap, in_=src_t.ap, func=AF.Square, accum_out=sst[:, 0:1]),
                 reads=[src_t], writes=[sst], pwrites=[junk_t])
            S.op("pool", lambda e: e.tensor_scalar(out=sst[:, 1:2], in0=sst[:, 0:1], scalar1=1.0 / D, scalar2=EPS,
                                                   op0=ALU.mult, op1=ALU.add), reads=[sst], writes=[sst])
            S.op("pool", lambda e: e.tensor_tensor(out=sst[:, 2:3], in0=sst[:, 1:2], in1=neghalf.ap, op=ALU.pow),
                 reads=[sst, neghalf], writes=[sst])
            return sst

        def norm_transposes(load_src, r0, scw, jsh, b):
            mark = None
            for j in range(4):
                xt = XR.next()
                o_ = S.dma("sp", xt, xt.ap, load_src[0], load_src[1][r0 + j * 128:r0 + (j + 1) * 128, :],
                           key=xt.name)
                if mark is None:
                    mark = o_
                sst = rms_rows(xt, xn[:, j, :], xn)
                S.op("dve", lambda e, xt=xt, sst=sst, j=j: e.tensor_scalar(
                    out=xn[:, j, :], in0=xt.ap, scalar1=sst[:, 2:3], scalar2=None, op0=ALU.mult),
                    reads=[xt, sst], pwrites=[xn])
            for dt in range(8):
                pb = PB[dt % 2]
                for j in range(4):
                    S.op("pe", lambda e, pb=pb, j=j, dt=dt: e.transpose(
                        out=pb[:, j * 128:(j + 1) * 128], in_=xn[:, j, dt * 128:(dt + 1) * 128], identity=identb.ap),
                        reads=[xn, identb], pwrites=[pb])
                if dt % 2 == 0:
                    S.op("dve", lambda e, pb=pb, dt=dt: e.tensor_scalar(
                        out=uT[:, dt, :], in0=pb[:, 0:512], scalar1=scp[scw][:, b, dt:dt + 1],
                        scalar2=modv[jsh][:, b, dt:dt + 1], op0=ALU.mult, op1=ALU.add),
                        reads=[pb, scp[scw], modv[jsh]], pwrites=[uT])
                else:
                    S.op("act", lambda e, pb=pb, dt=dt: e.activation(
                        out=uT[:, dt, :], in_=pb[:, 0:512], func=AF.Identity, scale=scp[scw][:, b, dt:dt + 1],
                        bias=modv[jsh][:, b, dt:dt + 1]), reads=[pb, scp[scw], modv[jsh]], pwrites=[uT])
            return mark

        def load_w(ring, src_t, src_ap, col0, ncols=512):
            slot = ring.next()
            S.dma("sp", slot, slot[:, :, 0:ncols],
                  src_t, src_ap[:, col0:col0 + ncols].rearrange("(k p) n -> p k n", p=128), key=slot.name)
            return slot

        for s in range(nseq):
            b = s
            S.op("pool", lambda e: e.memset(Sf.ap, 0.0), writes=[Sf])
            S.op("pool", lambda e: e.memset(Sbf[0].ap, 0.0), writes=[Sbf[0]])
            for t in range(4):
                S.op("pool", lambda e, t=t: e.memset(vglu[t][:, 0:30], 0.0), writes=[vglu[t]])
            for i in range(NTILE):
                r0 = s * tseq + i * NT
                first = (s == 0 and i == 0)
                mark = norm_transposes((None, x), r0, 0, 0, b)
                ti = s * NTILE + i
                ntt = nseq * NTILE
                t_bg = min(2, ntt - 1)
                if ti >= t_bg:
                    lo = (ti - t_bg) * len(ffn_casts) // (ntt - t_bg)
                    hi = (ti - t_bg + 1) * len(ffn_casts) // (ntt - t_bg)
                    for (tt_, oap_, iap_, key_) in ffn_casts[lo:hi]:
                        S.dma("pool", tt_, oap_, None, iap_, key=key_, partial=True, after=[mark])
                if ti == t_bg:
                    adaln([3, 4], after=[mark])
                    make_scp(1, 4, N2G)
                    adaln([5], after=[mark])
                if first:
                    tap("uT", uT, uT.ap, [P, 8, 512], BF16)
                slot = load_w(WI, t_win[5], win_bf, 2560)
                for t in range(4):
                    Z = PFc.next()
                    for kt in range(8):
                        S.op("pe", lambda e, Z=Z, slot=slot, kt=kt, t=t: e.matmul(
                            Z.ap, lhsT=slot[:, kt, t * 128:(t + 1) * 128], rhs=uT[:, kt, :], start=(kt == 0),
                            stop=(kt == 7)), reads=[slot, uT], writes=[Z] if kt == 0 else [], pwrites=[] if kt == 0 else [Z])
                    S.op("act", lambda e, Z=Z, t=t: e.activation(out=sigb4[:, t, :], in_=Z.ap, func=AF.Tanh, scale=0.5),
                         reads=[Z], pwrites=[sigb4])
                slot = load_w(WI, t_win[4], win_bf, 2048)
                for t in range(4):
                    Z = PFc.next()
                    for kt in range(8):
                        S.op("pe", lambda e, Z=Z, slot=slot, kt=kt, t=t: e.matmul(
                            Z.ap, lhsT=slot[:, kt, t * 128:(t + 1) * 128], rhs=uT[:, kt, :], start=(kt == 0),
                            stop=(kt == 7)), reads=[slot, uT], writes=[Z] if kt == 0 else [], pwrites=[] if kt == 0 else [Z])
                    vg = vglu[t]
                    S.op("dve", lambda e, Z=Z, t=t, vg=vg: e.scalar_tensor_tensor(
                        out=vg[:, 30:542], in0=sigb4[:, t, :], scalar=1.0, in1=Z.ap, op0=ALU.add, op1=ALU.mult),
                         reads=[Z, sigb4], writes=[vg])
                for t in range(4):
                    vg = vglu[t]
                    Dp = PFc.next()
                    for k in range(31):
                        S.op("pe", lambda e, Dp=Dp, t=t, k=k, vg=vg: e.matmul(
                            Dp.ap, lhsT=cm[:, t * 31 + k, :], rhs=vg[:, k:k + 512], start=(k == 0), stop=(k == 30)),
                            reads=[cm, vg], writes=[Dp] if k == 0 else [], pwrites=[] if k == 0 else [Dp])
                    S.op("pool", lambda e, vg=vg: e.tensor_copy(out=vg[:, 0:30], in_=vg[:, 512:542]),
                         reads=[vg], writes=[vg])
                    dsb = RFc.next()
                    S.op("act", lambda e, Dp=Dp, dsb=dsb, t=t: e.activation(
                        out=dsb.ap, in_=Dp.ap, func=AF.Identity, bias=bcv[:, t:t + 1]), reads=[Dp, bcv], writes=[dsb])
                    dsq = RBFc.next()
                    S.op("act", lambda e, Dp=Dp, dsq=dsq, t=t: e.activation(
                        out=dsq.ap, in_=Dp.ap, func=AF.Square, bias=bcv[:, t:t + 1]), reads=[Dp, bcv], writes=[dsq])
                    Vp = PFc.next()
                    S.op("pe", lambda e, Vp=Vp, dsq=dsq: e.matmul(Vp.ap, lhsT=Gmatb.ap, rhs=dsq.ap, start=True,
                                                                  stop=True), reads=[Gmatb, dsq], writes=[Vp])
                    rc = RFc.next()
                    S.op("act", lambda e, Vp=Vp, rc=rc: e.activation(out=rc.ap, in_=Vp.ap, func=AF.Ln,
                                                                    bias=epsT.ap), reads=[Vp, epsT], writes=[rc])
                    S.op("act", lambda e, rc=rc: e.activation(out=rc.ap, in_=rc.ap, func=AF.Exp, scale=-0.5),
                         reads=[rc], writes=[rc])
                    S.op("pool", lambda e, rc=rc, dsb=dsb: e.tensor_tensor(out=dsb.ap, in0=dsb.ap, in1=rc.ap,
                                                                          op=ALU.mult), reads=[rc, dsb], writes=[dsb])
                    S.op("act", lambda e, dsb=dsb, t=t: e.activation(
                        out=ovT[:, 4 + t, :], in_=dsb.ap, func=AF.Silu, scale=vc(CNG + t), bias=vc(CNB + t)),
                        reads=[dsb, vecs], pwrites=[ovT])
                slot = load_w(WI, t_win[1], win_bf, 512)
                for h in range(4):
                    Z = PFh.next()
                    for kt in range(8):
                        S.op("pe", lambda e, Z=Z, slot=slot, kt=kt, h=h: e.matmul(
                            Z.ap, lhsT=slot[:, kt, h * 128:(h + 1) * 128], rhs=uT[:, kt, :], start=(kt == 0),
                            stop=(kt == 7)), reads=[slot, uT], writes=[Z] if kt == 0 else [], pwrites=[] if kt == 0 else [Z])
                    sig = RF.next()
                    S.op("act", lambda e, Z=Z, sig=sig: e.activation(out=sig.ap, in_=Z.ap, func=AF.Tanh, scale=0.5),
                         reads=[Z], writes=[sig])
                    logf = RF.next()
                    S.op("act", lambda e, sig=sig, logf=logf, h=h: e.activation(
                        out=logf.ap, in_=sig.ap, func=AF.Ln, scale=a1[:, h:h + 1], bias=a2[:, h:h + 1]),
                        reads=[sig, a1, a2], writes=[logf])
                    bcum = RF.next()
                    S.op("dve", lambda e, bcum=bcum, logf=logf: e.tensor_tensor_scan(
                        out=bcum.ap, data0=m512.ap, data1=logf.ap, initial=0.0, op0=ALU.mult, op1=ALU.add),
                        reads=[m512, logf], writes=[bcum])
                    kk = RF.next()
                    S.op("pool", lambda e, kk=kk, sig=sig, h=h: e.tensor_scalar(
                        out=kk.ap, in0=sig.ap, scalar1=a3[:, h:h + 1], scalar2=a1[:, h:h + 1], op0=ALU.mult,
                        op1=ALU.add), reads=[sig, a3, a1], writes=[kk])
                    S.op("act", lambda e, bcum=bcum, h=h: e.activation(out=E4[:, h, :], in_=bcum.ap, func=AF.Exp),
                         reads=[bcum], pwrites=[E4])
                    einv = RF.next()
                    S.op("act", lambda e, bcum=bcum, einv=einv: e.activation(out=einv.ap, in_=bcum.ap, func=AF.Exp,
                                                                            scale=-1.0), reads=[bcum], writes=[einv])
                    S.op("pool", lambda e, h=h: e.tensor_copy(
                        out=eb[:, :, h], in_=E4[:, h, :].rearrange("p (c t) -> p c t", t=64)[:, :, 63]),
                        reads=[E4], pwrites=[eb])
                    S.op("pool", lambda e, kk=kk, einv=einv, h=h: e.tensor_tensor(
                        out=KTt[:, h, :], in0=kk.ap, in1=einv.ap, op=ALU.mult), reads=[kk, einv], pwrites=[KTt])
                    if first and h == 0:
                        tap("bcum0", bcum, bcum.ap, [P, 512])
                slot = load_w(WI, t_win[0], win_bf, 0)
                for h in range(4):
                    Z = PFh.next()
                    for kt in range(8):
                        S.op("pe", lambda e, Z=Z, slot=slot, kt=kt, h=h: e.matmul(
                            Z.ap, lhsT=slot[:, kt, h * 128:(h + 1) * 128], rhs=uT[:, kt, :], start=(kt == 0),
                            stop=(kt == 7)), reads=[slot, uT], writes=[Z] if kt == 0 else [], pwrites=[] if kt == 0 else [Z])
                    S.op("dve", lambda e, Z=Z, h=h: e.tensor_tensor(out=QT[:, h, :], in0=Z.ap, in1=E4[:, h, :],
                                                                    op=ALU.mult), reads=[Z, E4], pwrites=[QT])
                slot = load_w(WI, t_win[2], win_bf, 1024)
                for j in range(4):
                    Z = PFh.next()
                    for kt in range(8):
                        S.op("pe", lambda e, Z=Z, slot=slot, kt=kt, j=j: e.matmul(
                            Z.ap, lhsT=uT[:, kt, j * 128:(j + 1) * 128], rhs=slot[:, kt, :], start=(kt == 0),
                            stop=(kt == 7)), reads=[slot, uT], writes=[Z] if kt == 0 else [], pwrites=[] if kt == 0 else [Z])
                    S.op("dve", lambda e, Z=Z, j=j: e.tensor_copy(out=Vsb[:, j, :], in_=Z.ap),
                         reads=[Z], pwrites=[Vsb])
                slot = load_w(WI, t_win[3], win_bf, 1536)
                for h in range(4):
                    Z = PFh.next()
                    for kt in range(8):
                        S.op("pe", lambda e, Z=Z, slot=slot, kt=kt, h=h: e.matmul(
                            Z.ap, lhsT=slot[:, kt, h * 128:(h + 1) * 128], rhs=uT[:, kt, :], start=(kt == 0),
                            stop=(kt == 7)), reads=[slot, uT], writes=[Z] if kt == 0 else [], pwrites=[] if kt == 0 else [Z])
                    sg = RF.next()
                    S.op("act", lambda e, Z=Z, sg=sg: e.activation(out=sg.ap, in_=Z.ap, func=AF.Silu),
                         reads=[Z], writes=[sg])
                    S.op("pool", lambda e, sg=sg, h=h: e.tensor_scalar(
                        out=gz[:, h, :], in0=sg.ap, scalar1=vc(HNG), scalar2=1.0, op0=ALU.mult, op1=ALU.mult),
                        reads=[sg, vecs], pwrites=[gz])
                for j in range(4):
                    pb = PB[j % 2]
                    for h in range(4):
                        S.op("pe", lambda e, pb=pb, j=j, h=h: e.transpose(
                            out=pb[:, h * 128:(h + 1) * 128], in_=KTt[:, h, j * 128:(j + 1) * 128],
                            identity=identb.ap), reads=[KTt, identb], pwrites=[pb])
                    S.op("dve", lambda e, pb=pb, j=j: e.tensor_copy(
                        out=Ktok[:, j, :, :].rearrange("p h k -> p (h k)"), in_=pb[:, 0:512]),
                        reads=[pb], pwrites=[Ktok])
                if i > 0:
                    S.op("pool", lambda e: e.tensor_copy(out=Sbf[0].ap, in_=Sbf[8].ap), reads=[Sbf[8]],
                         writes=[Sbf[0]])
                for c in range(8):
                    j, pbase = c // 2, (c % 2) * 64
                    DS = PFh.next()
                    for h in range(4):
                        S.op("pe", lambda e, DS=DS, j=j, pbase=pbase, h=h: e.matmul(
                            DS[:, h * 128:(h + 1) * 128], lhsT=Ktok[pbase:pbase + 64, j, h, :],
                            rhs=Vsb[pbase:pbase + 64, j, h * 128:(h + 1) * 128], start=True, stop=True),
                            reads=[Ktok, Vsb], pwrites=[DS])
                    S.op("dve", lambda e, DS=DS: e.tensor_tensor(
                        out=tmpS.ap, in0=DS.ap.rearrange("p (h v) -> p h v", h=4), in1=Sf.ap, op=ALU.add),
                        reads=[DS, Sf], writes=[tmpS])
                    S.op("dve", lambda e, c=c: e.tensor_tensor(
                        out=Sf.ap, in0=tmpS.ap, in1=eb[:, c, :].unsqueeze(2).broadcast_to([P, 4, 128]), op=ALU.mult),
                        reads=[tmpS, eb], writes=[Sf])
                    S.op("act", lambda e, c=c: e.activation(
                        out=Sbf[c + 1].ap.rearrange("p h v -> p (h v)"), in_=Sf.ap.rearrange("p h v -> p (h v)"),
                        func=AF.Copy), reads=[Sf], writes=[Sbf[c + 1]])
                for h in range(4):
                    A = PFh.next()
                    for j in range(4):
                        S.op("pe", lambda e, A=A, j=j, h=h: e.matmul(
                            A[:, j * 128:(j + 1) * 128], lhsT=KTt[:, h, j * 128:(j + 1) * 128],
                            rhs=QT[:, h, j * 128:(j + 1) * 128], start=True, stop=True),
                            reads=[KTt, QT], pwrites=[A])
                    S.op("dve", lambda e, A=A, h=h: e.tensor_tensor(
                        out=Abf[:, h, :, :], in0=A.ap.rearrange("p (j t) -> p j t", j=4),
                        in1=maskA.ap.unsqueeze(1).broadcast_to([P, 4, 128]), op=ALU.mult),
                        reads=[A, maskA], pwrites=[Abf])
                for j in range(4):
                    O = PFh.next()
                    for h in range(4):
                        S.op("pe", lambda e, O=O, j=j, h=h: e.matmul(
                            O[:, h * 128:(h + 1) * 128], lhsT=Vsb[:, j, h * 128:(h + 1) * 128], rhs=Abf[:, h, j, :],
                            start=True, stop=False), reads=[Vsb, Abf], pwrites=[O])
                        for cc in range(2):
                            c = 2 * j + cc
                            S.op("pe", lambda e, O=O, j=j, h=h, cc=cc, c=c: e.matmul(
                                O[:, h * 128 + cc * 64:h * 128 + cc * 64 + 64], lhsT=Sbf[c][:, h, :],
                                rhs=QT[:, h, j * 128 + cc * 64:j * 128 + cc * 64 + 64], start=False, stop=(cc == 1)),
                                reads=[Sbf[c], QT], pwrites=[O])
                    osb = RF.next()
                    S.op("act", lambda e, O=O, osb=osb: e.activation(out=osb.ap, in_=O.ap, func=AF.Copy),
                         reads=[O], writes=[osb])
                    osq = RBF.next()
                    S.op("act", lambda e, O=O, osq=osq: e.activation(out=osq.ap, in_=O.ap, func=AF.Square),
                         reads=[O], writes=[osq])
                    MS = PFh.next()
                    S.op("pe", lambda e, MS=MS, osq=osq: e.matmul(MS.ap, lhsT=onesmb.ap, rhs=osq.ap, start=True,
                                                                  stop=True), reads=[onesmb, osq], writes=[MS])
                    ro = RF.next()
                    S.op("act", lambda e, MS=MS, ro=ro: e.activation(out=ro.ap, in_=MS.ap, func=AF.Ln,
                                                                    bias=epsT.ap), reads=[MS, epsT], writes=[ro])
                    S.op("act", lambda e, ro=ro: e.activation(out=ro.ap, in_=ro.ap, func=AF.Exp, scale=-0.5),
                         reads=[ro], writes=[ro])
                    S.op("pool", lambda e, ro=ro, osb=osb: e.tensor_tensor(out=osb.ap, in0=osb.ap, in1=ro.ap,
                                                                          op=ALU.mult), reads=[ro, osb], writes=[osb])
                    S.op("pool", lambda e, osb=osb, j=j: e.tensor_tensor(
                        out=ovT[:, 0:4, j * 128:(j + 1) * 128], in0=osb.ap.rearrange("p (h t) -> p h t", h=4),
                        in1=gz[:, :, j * 128:(j + 1) * 128], op=ALU.mult), reads=[osb, gz], pwrites=[ovT])
                if first:
                    tap("ovT", ovT, ovT.ap, [P, 8, 512], BF16)
                wo = [load_w(WI, t_wout, wout_bf, half * 512) for half in range(2)]
                g1t = gbc[("g1", b)]
                for j in range(4):
                    xr = XO.next()
                    S.dma("sp", xr, xr.ap, None, x[r0 + j * 128:r0 + (j + 1) * 128, :], key=xr.name)
                    for half in range(2):
                        slot = wo[half]
                        M = PFh.next()
                        for kt in range(8):
                            S.op("pe", lambda e, M=M, slot=slot, kt=kt, j=j: e.matmul(
                                M.ap, lhsT=ovT[:, kt, j * 128:(j + 1) * 128], rhs=slot[:, kt, :], start=(kt == 0),
                                stop=(kt == 7)), reads=[slot, ovT], writes=[M] if kt == 0 else [],
                                pwrites=[] if kt == 0 else [M])
                        tmp = RF.next()
                        S.op("dve", lambda e, M=M, tmp=tmp, half=half, g1t=g1t: e.tensor_tensor(
                            out=tmp.ap, in0=M.ap, in1=g1t[:, half * 512:(half + 1) * 512], op=ALU.mult),
                            reads=[M, g1t], writes=[tmp])
                        S.op("pool", lambda e, tmp=tmp, xr=xr, half=half: e.tensor_tensor(
                            out=xr[:, half * 512:(half + 1) * 512], in0=xr[:, half * 512:(half + 1) * 512],
                            in1=tmp.ap, op=ALU.add), reads=[tmp, xr], writes=[xr])
                    S.dma("sp", t_h1, h1[r0 + j * 128:r0 + (j + 1) * 128, :], xr, xr.ap, key="h1st_" + xr.name, partial=True)
                    if first and j == 0:
                        tap("h1_0", xr, xr.ap, [P, 1024])
        S.barrier()

        off["f"], off["b"] = persist_f, persist_b
        for b in range(nseq):
            gbc[("g2", b)] = fa([1024], "g2bc%d" % b)
        fgbc = fa([1024], "fgbc")
        S.dma("sp", fgbc, fgbc.ap, None, final_norm_g.partition_broadcast(P), key="fgbc")
        dgB = [fa([128], "dgb%d" % i) for i in range(2)]
        make_gbc("g2", 5, dgB)
        XR2 = Ring([fa([1024], "xq%d" % i) for i in range(3)])
        XO2 = Ring([fa([1024], "xp%d" % i) for i in range(2)])
        PFg = Ring(PFl[0:2])
        PFv = Ring(PFl[2:4])
        PFy = Ring(PFl[4:6])
        xn2 = ba([4, 1024], "xn2")
        uT2 = ba([8, 512], "u2T")
        wdn = ba([NFT, 1024], "wdn")
        WG = Ring([ba([8, 512], "wg%d" % i) for i in range(3)])
        actT = ba([NFT, 512], "actT")
        GB = Ring([fa([514], "gb%d" % i) for i in range(3)])
        RF2 = Ring([fa([512], "rg%d" % i) for i in range(8)])
        ghist = [fa([2], "ghist%d" % i) for i in range(NFT)]
        ssq2 = Ring([fa([4], "ssr%d" % i) for i in range(4)])
        ssqF = Ring([fa([4], "ssf%d" % i) for i in range(4)])
        YO = Ring([fa([1024], "yo%d" % i) for i in range(2)])
        for kt in range(NFT):
            r1 = min(DFF, (kt + 1) * 128)
            S.dma("sp", wdn, wdn[0:r1 - kt * 128, kt, :], t_wdn, wdn_bf[kt * 128:r1, :], key="wdn", partial=True)

        for s in range(nseq):
            b = s
            for ft in range(NFT):
                S.op("pool", lambda e, ft=ft: e.memset(ghist[ft].ap, 0.0), writes=[ghist[ft]])
            for i in range(NTILE):
                r0 = s * tseq + i * NT
                first = (s == 0 and i == 0)
                hts = []
                for j in range(4):
                    xt = XR2.next()
                    S.dma("sp", xt, xt.ap, t_h1, h1[r0 + j * 128:r0 + (j + 1) * 128, :], key=xt.name)
                    hts.append(xt)
                    sst = ssq2.next()
                    S.op("act", lambda e, xt=xt, sst=sst, j=j: e.activation(out=xn2[:, j, :], in_=xt.ap, func=AF.Square,
                                                                             accum_out=sst[:, 0:1]),
                         reads=[xt], writes=[sst], pwrites=[xn2])
                    S.op("pool", lambda e, sst=sst: e.tensor_scalar(out=sst[:, 1:2], in0=sst[:, 0:1], scalar1=1.0 / D,
                                                                    scalar2=EPS, op0=ALU.mult, op1=ALU.add),
                         reads=[sst], writes=[sst])
                    S.op("pool", lambda e, sst=sst: e.tensor_tensor(out=sst[:, 2:3], in0=sst[:, 1:2], in1=neghalf.ap,
                                                                    op=ALU.pow), reads=[sst, neghalf], writes=[sst])
                    S.op("dve", lambda e, xt=xt, sst=sst, j=j: e.tensor_scalar(
                        out=xn2[:, j, :], in0=xt.ap, scalar1=sst[:, 2:3], scalar2=None, op0=ALU.mult),
                        reads=[xt, sst], pwrites=[xn2])
                for dt in range(8):
                    pb = PB[dt % 2]
                    for j in range(4):
                        S.op("pe", lambda e, pb=pb, j=j, dt=dt: e.transpose(
                            out=pb[:, j * 128:(j + 1) * 128], in_=xn2[:, j, dt * 128:(dt + 1) * 128],
                            identity=identb.ap), reads=[xn2, identb], pwrites=[pb])
                    if dt % 2 == 0:
                        S.op("dve", lambda e, pb=pb, dt=dt, b=b: e.tensor_scalar(
                            out=uT2[:, dt, :], in0=pb[:, 0:512], scalar1=scp[1][:, b, dt:dt + 1],
                            scalar2=modv[3][:, b, dt:dt + 1], op0=ALU.mult, op1=ALU.add),
                            reads=[pb, scp[1], modv[3]], pwrites=[uT2])
                    else:
                        S.op("act", lambda e, pb=pb, dt=dt, b=b: e.activation(
                            out=uT2[:, dt, :], in_=pb[:, 0:512], func=AF.Identity, scale=scp[1][:, b, dt:dt + 1],
                            bias=modv[3][:, b, dt:dt + 1]), reads=[pb, scp[1], modv[3]], pwrites=[uT2])
                for g2 in range(11):
                    ncol = 256 if g2 < 10 else 192
                    slot = WG.next()
                    S.dma("sp", slot, slot[:, :, 0:ncol], t_wgu,
                          wgu_bf[:, g2 * 256:g2 * 256 + ncol].rearrange("(k p) n -> p k n", p=128), key=slot.name,
                          partial=True)
                    S.dma("sp", slot, slot[:, :, 256:256 + ncol], t_wgu,
                          wgu_bf[:, DFF + g2 * 256:DFF + g2 * 256 + ncol].rearrange("(k p) n -> p k n", p=128),
                          key=slot.name, partial=True)
                    for sub in range(2):
                        ft = g2 * 2 + sub
                        if ft >= NFT:
                            continue
                        m = 128 if ft < NFT - 1 else 64
                        G = PFg.next()
                        Vv = PFv.next()
                        for kt in range(8):
                            S.op("pe", lambda e, G=G, slot=slot, kt=kt, sub=sub, m=m: e.matmul(
                                G[0:m, :], lhsT=slot[:, kt, sub * 128:sub * 128 + m], rhs=uT2[:, kt, :],
                                start=(kt == 0), stop=(kt == 7)), reads=[slot, uT2],
                                writes=[G] if kt == 0 else [], pwrites=[] if kt == 0 else [G])
                        for kt in range(8):
                            S.op("pe", lambda e, Vv=Vv, slot=slot, kt=kt, sub=sub, m=m: e.matmul(
                                Vv[0:m, :], lhsT=slot[:, kt, 256 + sub * 128:256 + sub * 128 + m], rhs=uT2[:, kt, :],
                                start=(kt == 0), stop=(kt == 7)), reads=[slot, uT2],
                                writes=[Vv] if kt == 0 else [], pwrites=[] if kt == 0 else [Vv])
                        gb = GB.next()
                        acc = RF2.next()
                        S.op("act", lambda e, G=G, acc=acc, ft=ft, m=m: e.activation(
                            out=acc[0:m, :], in_=G[0:m, :], func=AF.Identity, scale=vecs[0:m, FCW + 44 + ft:FCW + 44 + ft + 1],
                            bias=vecs[0:m, FCB + ft:FCB + ft + 1]), reads=[G, vecs], writes=[acc])
                        S.op("pool", lambda e, gb=gb, ft=ft, m=m: e.tensor_copy(out=gb[0:m, 0:2], in_=ghist[ft][0:m, :]),
                             reads=[ghist[ft]], writes=[gb])
                        S.op("act", lambda e, gb=gb, G=G, m=m: e.activation(out=gb[0:m, 2:514], in_=G[0:m, :],
                                                                           func=AF.Copy), reads=[G], pwrites=[gb])
                        S.op("pool", lambda e, gb=gb, ft=ft, m=m: e.tensor_copy(out=ghist[ft][0:m, :],
                                                                               in_=gb[0:m, 512:514]),
                             reads=[gb], writes=[ghist[ft]])
                        S.op("dve", lambda e, gb=gb, acc=acc, ft=ft, m=m: e.scalar_tensor_tensor(
                            out=acc[0:m, :], in0=gb[0:m, 1:513], scalar=vecs[0:m, FCW + 22 + ft:FCW + 22 + ft + 1],
                            in1=acc[0:m, :], op0=ALU.mult, op1=ALU.add), reads=[gb, acc, vecs], writes=[acc])
                        S.op("dve", lambda e, gb=gb, acc=acc, ft=ft, m=m: e.scalar_tensor_tensor(
                            out=acc[0:m, :], in0=gb[0:m, 0:512], scalar=vecs[0:m, FCW + ft:FCW + ft + 1],
                            in1=acc[0:m, :], op0=ALU.mult, op1=ALU.add), reads=[gb, acc, vecs], writes=[acc])
                        S.op("act", lambda e, acc=acc, m=m: e.activation(out=acc[0:m, :], in_=acc[0:m, :],
                                                                        func=AF.Gelu), reads=[acc], writes=[acc])
                        S.op("dve", lambda e, Vv=Vv, acc=acc, ft=ft, m=m: e.tensor_tensor(
                            out=actT[0:m, ft, :], in0=Vv[0:m, :], in1=acc[0:m, :], op=ALU.mult),
                            reads=[Vv, acc], pwrites=[actT])
                if first:
                    tap("actT", actT, actT.ap, [P, NFT, 512], BF16)
                for j in range(4):
                    ht = XO2.next()
                    S.dma("sp", ht, ht.ap, t_h1, h1[r0 + j * 128:r0 + (j + 1) * 128, :], key=ht.name)
                    for half in range(2):
                        Y = PFy.next()
                        for kt in range(NFT):
                            kk_ = 128 if kt < NFT - 1 else 64
                            S.op("pe", lambda e, Y=Y, kt=kt, j=j, half=half, kk_=kk_: e.matmul(
                                Y.ap, lhsT=actT[0:kk_, kt, j * 128:(j + 1) * 128],
                                rhs=wdn[0:kk_, kt, half * 512:(half + 1) * 512], start=(kt == 0),
                                stop=(kt == NFT - 1)), reads=[actT, wdn], writes=[Y] if kt == 0 else [],
                                pwrites=[] if kt == 0 else [Y])
                        tmp = RF2.next()
                        S.op("dve", lambda e, Y=Y, tmp=tmp, half=half, b=b: e.tensor_tensor(
                            out=tmp.ap, in0=Y.ap, in1=gbc[("g2", b)][:, half * 512:(half + 1) * 512], op=ALU.mult),
                            reads=[Y, gbc[("g2", b)]], writes=[tmp])
                        S.op("pool", lambda e, tmp=tmp, ht=ht, half=half: e.tensor_tensor(
                            out=ht[:, half * 512:(half + 1) * 512], in0=ht[:, half * 512:(half + 1) * 512],
                            in1=tmp.ap, op=ALU.add), reads=[tmp, ht], writes=[ht])
                    sst = ssqF.next()
                    yo = YO.next()
                    S.op("act", lambda e, ht=ht, sst=sst, yo=yo: e.activation(out=yo.ap, in_=ht.ap, func=AF.Square,
                                                                               accum_out=sst[:, 0:1]),
                         reads=[ht], writes=[yo, sst])
                    S.op("pool", lambda e, sst=sst: e.tensor_scalar(out=sst[:, 1:2], in0=sst[:, 0:1], scalar1=1.0 / D,
                                                                    scalar2=EPS, op0=ALU.mult, op1=ALU.add),
                         reads=[sst], writes=[sst])
                    S.op("pool", lambda e, sst=sst: e.tensor_tensor(out=sst[:, 2:3], in0=sst[:, 1:2], in1=neghalf.ap,
                                                                    op=ALU.pow), reads=[sst, neghalf], writes=[sst])
                    S.op("dve", lambda e, ht=ht, sst=sst, yo=yo: e.scalar_tensor_tensor(
                        out=yo.ap, in0=ht.ap, scalar=sst[:, 2:3], in1=fgbc.ap, op0=ALU.mult, op1=ALU.mult),
                        reads=[ht, sst, fgbc], writes=[yo])
                    finals.append(S.dma("sp", None, y[r0 + j * 128:r0 + (j + 1) * 128, :], yo, yo.ap, key=yo.name))
        S.emit(final_wait_ops=finals)
    return nc, taps


def _prep_shared(inputs):
    sq = lambda a: np.ascontiguousarray(np.asarray(a, dtype=np.float32))
    return {
        "lb_table": sq(inputs["lb_table"]),
        "w_ada": sq(inputs["w_ada"][0]),
        "b_ada": sq(inputs["b_ada"][0]),
        "norm1_g": sq(inputs["norm1_g"][0]),
        "w_in": sq(inputs["w_in"][0]),
        "hgrn_norm_g": sq(inputs["hgrn_norm_g"][0]),
        "conv_w": sq(inputs["conv_w"][0]),
        "conv_b": sq(inputs["conv_b"][0]),
        "conv_norm_g": sq(inputs["conv_norm_g"][0]),
        "conv_norm_b": sq(inputs["conv_norm_b"][0]),
        "w_out": sq(inputs["w_out"][0]),
        "norm2_g": sq(inputs["norm2_g"][0]),
        "w_gu": sq(inputs["w_gu"][0]),
        "ffn_conv_w": sq(inputs["ffn_conv_w"][0]),
        "ffn_conv_b": sq(inputs["ffn_conv_b"][0]),
        "w_down": sq(inputs["w_down"][0]),
        "final_norm_g": sq(inputs["final_norm_g"]),
    }


def kernel(**inputs):
    x = np.asarray(inputs["x"], dtype=np.float32)
    c = np.asarray(inputs["c"], dtype=np.float32)
    B, Tn, Dm = x.shape
    nseq = B // NCORES
    nc = build(nseq=nseq, tseq=Tn)[0]
    shared = _prep_shared(inputs)
    in_maps = []
    for i in range(NCORES):
        m = dict(shared)
        m["x"] = np.ascontiguousarray(x[i * nseq:(i + 1) * nseq].reshape(nseq * Tn, Dm))
        m["c"] = np.ascontiguousarray(c[i * nseq:(i + 1) * nseq])
        in_maps.append(m)
    res = run_bass_kernel_spmd(nc, in_maps, core_ids=list(range(NCORES)))
    out = np.concatenate([r["y"].reshape(nseq, Tn, Dm) for r in res.results], axis=0)
    return out.astype(np.float32)
```
